# Optimizing a Trainium2 kernel written in Bass

```python
import jax, jax.numpy as jnp
from jax import lax
import numpy as np

D_MODEL = 1024
BATCH = 16
SEQ = 256
DEPTH = 2
DEC_BATCH = 8
DEC_SEQ = 4096
PAST_LEN = 256

GRID_W = 64
POOL_WINDOWS = (2, 4, 8, 16)
POOL_GROUPS = len(POOL_WINDOWS)
POOL_DIM = D_MODEL
POOL_GROUP_DIM = POOL_DIM // POOL_GROUPS
SSD_EXPAND = 2
D_INNER = SSD_EXPAND * D_MODEL
SSD_HEAD_DIM = 64
SSD_HEADS = D_INNER // SSD_HEAD_DIM
SSD_GROUPS = 4
SSD_STATE = 128
SSD_CONV = 4
SSD_CHUNK = 128
CONV_DIM = D_INNER + 2 * SSD_GROUPS * SSD_STATE
N_BRANCH = 2
IN_COLS = POOL_DIM + D_INNER + CONV_DIM + 2 * SSD_HEADS + N_BRANCH * D_MODEL
D_FF = 2816
N_MOD = 9
EPS = 1e-6

kernel_name = "hybrid_pool_ssd_diffusion_step"


def _rmsnorm(x, g):
    xf = x.astype(jnp.float32)
    y = xf * lax.rsqrt(jnp.mean(xf * xf, axis=-1, keepdims=True) + EPS)
    return (y * g.astype(jnp.float32)).astype(x.dtype)


def _modulate(x, g, shift, scale):
    return _rmsnorm(x, g) * (1 + scale) + shift


def _adaln(cond, w, b):
    mod = jax.nn.silu(cond) @ w + b
    return mod.reshape(cond.shape[0], N_MOD, 1, D_MODEL)


def _swiglu(h, w13, w2):
    g, u = jnp.split(h @ w13, 2, axis=-1)
    return (jax.nn.silu(g) * u) @ w2


def _dwconv_centred(u, w, b):
    K = w.shape[0]
    left = K // 2
    T = u.shape[1]
    up = jnp.pad(u, ((0, 0), (left, K - 1 - left), (0, 0)))
    out = b + up[:, 0:T] * w[0]
    for k in range(1, K):
        out = out + up[:, k:k + T] * w[k]
    return out


def _bounds(n, w):
    idx = jnp.arange(n)
    lo = jnp.clip(idx - w // 2, 0, n)
    hi = jnp.clip(idx - w // 2 + w, 0, n)
    return lo, hi


def _pool_seq(u, w):
    T = u.shape[1]
    s = jnp.pad(jnp.cumsum(u.astype(jnp.float32), axis=1), ((0, 0), (1, 0), (0, 0)))
    lo, hi = _bounds(T, w)
    cnt = (hi - lo).astype(jnp.float32)[None, :, None]
    return ((s[:, hi] - s[:, lo]) / cnt).astype(u.dtype)


def _pool_grid(u, w):
    R, W = u.shape[1], u.shape[2]
    s = jnp.cumsum(jnp.cumsum(u.astype(jnp.float32), axis=1), axis=2)
    s = jnp.pad(s, ((0, 0), (1, 0), (1, 0), (0, 0)))
    rlo, rhi = _bounds(R, w)
    clo, chi = _bounds(W, w)
    s_hi = s[:, rhi]
    s_lo = s[:, rlo]
    tot = (s_hi[:, :, chi] - s_hi[:, :, clo]) - (s_lo[:, :, chi] - s_lo[:, :, clo])
    cnt = ((rhi - rlo)[:, None] * (chi - clo)[None, :]).astype(jnp.float32)[None, :, :, None]
    return (tot / cnt).astype(u.dtype)


def _pool_mixer(u, pool_w, pool_scale, rows):
    b, T, _ = u.shape
    outs = []
    for g, w in enumerate(POOL_WINDOWS):
        ug = u[..., g * POOL_GROUP_DIM:(g + 1) * POOL_GROUP_DIM]
        if rows is None:
            pooled = _pool_seq(ug, w)
        else:
            pooled = _pool_grid(ug.reshape(b, rows, GRID_W, POOL_GROUP_DIM), w).reshape(b, T, POOL_GROUP_DIM)
        outs.append(pooled - ug)
    d = jnp.stack(outs, axis=2)
    y = jnp.einsum('btgc,gcd->btgd', d, pool_w).reshape(b, T, POOL_DIM)
    return y * pool_scale


def _ssd_scan(x, dt, a, bmat, cmat, h0):
    b, T, H, P = x.shape
    G, N = bmat.shape[2], bmat.shape[3]
    Hg = H // G
    L = SSD_CHUNK
    nc = T // L
    xf = (x.astype(jnp.float32) * dt[..., None]).reshape(b, nc, L, G, Hg, P)
    da = (dt * a).reshape(b, nc, L, G, Hg)
    bc = bmat.astype(jnp.float32).reshape(b, nc, L, G, N)
    cc = cmat.astype(jnp.float32).reshape(b, nc, L, G, N)
    cs = jnp.cumsum(da, axis=2)
    seg = cs[:, :, :, None] - cs[:, :, None]
    causal = jnp.tril(jnp.ones((L, L), dtype=bool))[:, :, None, None]
    decay = jnp.where(causal, jnp.exp(jnp.where(causal, seg, 0.0)), 0.0)
    scores = jnp.einsum('bclgn,bcsgn->bclsg', cc, bc)
    m = scores[..., None] * decay
    y_diag = jnp.einsum('bclsgh,bcsghp->bclghp', m, xf)
    xw = xf * jnp.exp(cs[:, :, -1:] - cs)[..., None]
    states = jnp.einsum('bclgn,bclghp->bcghpn', bc, xw)
    chunk_decay = jnp.exp(cs[:, :, -1])

    def step(h, inp):
        st, dec = inp
        return h * dec[..., None, None] + st, h

    h0g = h0.astype(jnp.float32).reshape(b, G, Hg, P, N)
    h_last, h_prev = lax.scan(step, h0g, (jnp.moveaxis(states, 1, 0), jnp.moveaxis(chunk_decay, 1, 0)))
    h_prev = jnp.moveaxis(h_prev, 0, 1)
    y_off = jnp.einsum('bclgn,bcghpn->bclghp', cc, h_prev) * jnp.exp(cs)[..., None]
    y = (y_diag + y_off).reshape(b, T, H, P)
    return y, h_last.reshape(b, H, P, N)


def _token_mixer(h, rows, h0_f, h0_b, lp):
    b, T, _ = h.shape
    proj = h @ lp['w_in']
    s1 = POOL_DIM
    s2 = s1 + D_INNER
    s3 = s2 + CONV_DIM
    s4 = s3 + 2 * SSD_HEADS
    u_pool, z, xbc, dt_raw, gates = jnp.split(proj, [s1, s2, s3, s4], axis=-1)
    y_pool = _pool_mixer(u_pool, lp['pool_w'], lp['pool_scale'], rows)
    xbc = jax.nn.silu(_dwconv_centred(xbc, lp['conv_w'], lp['conv_b']))
    xs, bm, cm = jnp.split(xbc, [D_INNER, D_INNER + SSD_GROUPS * SSD_STATE], axis=-1)
    xs = xs.reshape(b, T, SSD_HEADS, SSD_HEAD_DIM)
    bm = bm.reshape(b, T, SSD_GROUPS, SSD_STATE)
    cm = cm.reshape(b, T, SSD_GROUPS, SSD_STATE)
    dt = jax.nn.softplus(dt_raw.astype(jnp.float32).reshape(b, T, 2, SSD_HEADS) + lp['dt_bias'].astype(jnp.float32))
    a = -jnp.exp(lp['a_log'].astype(jnp.float32))
    y_f, hf = _ssd_scan(xs, dt[:, :, 0], a[0], bm, cm, h0_f)
    flip = lambda t: jnp.flip(t, axis=1)
    y_b, hb = _ssd_scan(flip(xs), flip(dt[:, :, 1]), a[1], flip(bm), flip(cm), h0_b)
    y = y_f + flip(y_b) + lp['d_skip'].astype(jnp.float32)[:, None] * xs.astype(jnp.float32)
    y = y.reshape(b, T, D_INNER).astype(h.dtype)
    y = _rmsnorm(y * jax.nn.silu(z), lp['ssd_norm'])
    g_a, g_b = jnp.split(jax.nn.sigmoid(gates), 2, axis=-1)
    merged = g_a * (y_pool @ lp['w_branch_pool']) + g_b * (y @ lp['w_branch_ssd'])
    return merged @ lp['w_out'], hf, hb


def _layer(x, mod, rows, h0_f, h0_b, lp):
    ng = lp['norm_g']
    x = x + 0.5 * mod[:, 2] * _swiglu(_modulate(x, ng[0], mod[:, 0], mod[:, 1]), lp['ffn1_w13'], lp['ffn1_w2'])
    m, hf, hb = _token_mixer(_modulate(x, ng[1], mod[:, 3], mod[:, 4]), rows, h0_f, h0_b, lp)
    x = x + mod[:, 5] * m
    x = x + 0.5 * mod[:, 8] * _swiglu(_modulate(x, ng[2], mod[:, 6], mod[:, 7]), lp['ffn2_w13'], lp['ffn2_w2'])
    return x, hf, hb


def setup_inputs(seed: int = 0) -> dict:
    key = jax.random.key(seed)
    ks = jax.random.split(key, 32)
    f32 = jnp.float32
    nrm = lambda k, shape, s: jax.random.normal(k, shape, f32) * s
    sd = (DEC_BATCH, DEPTH, SSD_HEADS, SSD_HEAD_DIM, SSD_STATE)
    dt0 = jnp.exp(jax.random.uniform(ks[17], (DEPTH, 2, SSD_HEADS), f32, np.log(1e-3), np.log(1e-1)))
    return {
        'x_prompt': nrm(ks[0], (BATCH, SEQ, D_MODEL), 1.0),
        'x_sample': nrm(ks[1], (DEC_BATCH, DEC_SEQ, D_MODEL), 1.0),
        'state_ssd_fwd': nrm(ks[2], sd, 0.5),
        'state_ssd_bwd': nrm(ks[3], sd, 0.5),
        'c': nrm(ks[4], (DEC_BATCH, D_MODEL), 1.0),
        'c_ctx': nrm(ks[5], (D_MODEL,), 1.0),
        'ada_w': nrm(ks[6], (DEPTH, D_MODEL, N_MOD * D_MODEL), D_MODEL ** -0.5),
        'ada_b': nrm(ks[7], (DEPTH, N_MOD * D_MODEL), 0.02),
        'norm_g': 1.0 + nrm(ks[8], (DEPTH, 3, D_MODEL), 0.02),
        'ffn1_w13': nrm(ks[9], (DEPTH, D_MODEL, 2 * D_FF), D_MODEL ** -0.5),
        'ffn1_w2': nrm(ks[10], (DEPTH, D_FF, D_MODEL), D_FF ** -0.5),
        'ffn2_w13': nrm(ks[11], (DEPTH, D_MODEL, 2 * D_FF), D_MODEL ** -0.5),
        'ffn2_w2': nrm(ks[12], (DEPTH, D_FF, D_MODEL), D_FF ** -0.5),
        'w_in': nrm(ks[13], (DEPTH, D_MODEL, IN_COLS), D_MODEL ** -0.5),
        'pool_w': nrm(ks[14], (DEPTH, POOL_GROUPS, POOL_GROUP_DIM, POOL_GROUP_DIM), POOL_GROUP_DIM ** -0.5),
        'pool_scale': 1.0 + nrm(ks[15], (DEPTH, POOL_DIM), 0.1),
        'conv_w': nrm(ks[16], (DEPTH, SSD_CONV, CONV_DIM), SSD_CONV ** -0.5),
        'conv_b': nrm(ks[18], (DEPTH, CONV_DIM), 0.02),
        'a_log': jnp.log(jax.random.uniform(ks[19], (DEPTH, 2, SSD_HEADS), f32, 1.0, 16.0)),
        'dt_bias': dt0 + jnp.log(-jnp.expm1(-dt0)),
        'd_skip': 1.0 + nrm(ks[20], (DEPTH, SSD_HEADS), 0.1),
        'ssd_norm': 1.0 + nrm(ks[21], (DEPTH, D_INNER), 0.02),
        'w_branch_pool': nrm(ks[22], (DEPTH, POOL_DIM, D_MODEL), POOL_DIM ** -0.5),
        'w_branch_ssd': nrm(ks[23], (DEPTH, D_INNER, D_MODEL), D_INNER ** -0.5),
        'w_out': nrm(ks[24], (DEPTH, D_MODEL, D_MODEL), D_MODEL ** -0.5),
        'final_norm': 1.0 + nrm(ks[25], (D_MODEL,), 0.02),
    }


def reference(x_prompt, x_sample, state_ssd_fwd, state_ssd_bwd, c, c_ctx, ada_w, ada_b, norm_g,
              ffn1_w13, ffn1_w2, ffn2_w13, ffn2_w2, w_in, pool_w, pool_scale, conv_w, conv_b,
              a_log, dt_bias, d_skip, ssd_norm, w_branch_pool, w_branch_ssd, w_out, final_norm):
    bp = x_prompt.shape[0]
    rows = x_sample.shape[1] // GRID_W
    zeros = jnp.zeros((bp, SSD_HEADS, SSD_HEAD_DIM, SSD_STATE), x_prompt.dtype)
    xc = x_prompt
    xl = x_sample
    new_f = []
    new_b = []
    for l in range(DEPTH):
        lp = {
            'norm_g': norm_g[l], 'ffn1_w13': ffn1_w13[l], 'ffn1_w2': ffn1_w2[l],
            'ffn2_w13': ffn2_w13[l], 'ffn2_w2': ffn2_w2[l], 'w_in': w_in[l],
            'pool_w': pool_w[l], 'pool_scale': pool_scale[l], 'conv_w': conv_w[l], 'conv_b': conv_b[l],
            'a_log': a_log[l], 'dt_bias': dt_bias[l], 'd_skip': d_skip[l], 'ssd_norm': ssd_norm[l],
            'w_branch_pool': w_branch_pool[l], 'w_branch_ssd': w_branch_ssd[l], 'w_out': w_out[l],
        }
        mod_ctx = _adaln(c_ctx[None, :], ada_w[l], ada_b[l])
        mod_lat = _adaln(c, ada_w[l], ada_b[l])
        xc, hf, hb = _layer(xc, mod_ctx, None, zeros, zeros, lp)
        new_f.append(hf.astype(x_prompt.dtype))
        new_b.append(hb.astype(x_prompt.dtype))
        xl, _, _ = _layer(xl, mod_lat, rows, state_ssd_fwd[:, l], state_ssd_bwd[:, l], lp)
    y_prompt = _rmsnorm(xc, final_norm)
    y_sample = _rmsnorm(xl, final_norm)
    new_state_fwd = jnp.stack(new_f, axis=1)
    new_state_bwd = jnp.stack(new_b, axis=1)
    return (y_prompt, y_sample, new_state_fwd, new_state_bwd)
```

```python
import contextlib
import numpy as np
import concourse.bass as bass
import concourse.mybir as mybir
from concourse.bass_utils import run_bass_kernel_spmd

F32 = mybir.dt.float32
BF16 = mybir.dt.bfloat16
AF = mybir.ActivationFunctionType
ALU = mybir.AluOpType
EPS = 1e-6
POOL_WINDOWS = (2, 4, 8, 16)


class Cfg:
    def __init__(s, D=1024, DFF=2816, TS=4096, TP=256, L=2):
        s.D = D; s.KD = D // 128; s.DFF = DFF; s.KF = DFF // 128
        s.DI = 2 * D; s.H = s.DI // 64; s.G = 4; s.HG = s.H // 4; s.GW = s.HG * 64
        s.CONVD = s.DI + 2 * 4 * 128; s.KC = s.CONVD // 128; s.KX = s.DI // 128; s.KG = 2 * s.KD
        s.GD = D // 4; s.GC = s.GD // 128
        s.TS = TS; s.TP = TP; s.L = L
        s.NBS = TS // 512; s.NB = s.NBS + 1
        s.NCS = TS // 128; s.NCH = s.NCS + 4
        s.ROWS = TS // 64
        assert 2 * TP == 512 and s.GD % 128 == 0 and s.KF % 2 == 0
        s.TOKP = TS + 4 + 2 * (TP + 4)
        s.seqs = [(0, s.NCS, 0, 'S'), (s.NCS, 2, TS + 4, 'P'), (s.NCS + 2, 2, TS + 4 + TP + 4, 'P')]
        s.pieces = ([('up', i) for i in range(D // 512)] + [('z', i) for i in range(s.DI // 512)] +
                    [('xbc', i) for i in range(s.CONVD // 512)] + [('gt', i) for i in range(2 * D // 512)])
        s.NADA = 9 * D // 512
        s.VL = 9 * s.KD + 3 * s.KD + s.KD + 4 * s.KC + s.KC + s.KX
        s.V_ADAB = 0; s.V_NG = 9 * s.KD; s.V_PS = s.V_NG + 3 * s.KD; s.V_CW = s.V_PS + s.KD
        s.V_CB = s.V_CW + 4 * s.KC; s.V_SN = s.V_CB + s.KC
        s.V_C = L * s.VL; s.V_CCTX = s.V_C + s.KD; s.V_FN = s.V_CCTX + s.KD; s.RV = s.V_FN + s.KD
        s.RB = 5 * s.H
        s.dl_s = [[-1, 0], [-1, 0, 1], [-2, -1, 0, 1, 2], [-4, -3, -2, -1, 0, 1, 2, 3, 4]]
        s.dl_p = [[-1, 0], [-1, 0, 1], [-1, 0, 1], [-1, 0, 1]]
        s.mi_s = {}; s.mi_p = {}
        n = 1
        for g in range(4):
            for d in s.dl_s[g]:
                s.mi_s[(g, d)] = n; n += 1
        for g in range(4):
            for d in s.dl_p[g]:
                s.mi_p[(g, d)] = n; n += 1
        s.K_GT = n; s.K_LT = n + 1; s.K_ONE = n + 2; s.K_NEGF = n + 3; s.K_NEGB = n + 7
        s.NCB = n + 11
        s.C_LE = 0; s.C_GT = 128; s.C_GE = 256; s.C_LT = 384; s.C_ONE = 512
        s.C_INVS = 640; s.C_INVP = s.C_INVS + 4 * 128; s.C_NEG = s.C_INVP + 2 * 4 * 128
        s.C_IDF = s.C_NEG + (s.NCS + 2) * 4
        s.NCF = s.C_IDF + 128

    def seq_of(s, ch):
        for i, (c0, n, off, kind) in enumerate(s.seqs):
            if c0 <= ch < c0 + n:
                return i, ch - c0
        raise ValueError


def _bounds(n, w):
    idx = np.arange(n)
    lo = np.clip(idx - w // 2, 0, n)
    hi = np.clip(idx - w // 2 + w, 0, n)
    return lo, hi


def host_consts(cfg):
    cf = np.zeros((128, cfg.NCF), np.float32)
    i = np.arange(128)
    lp, l = i[:, None], i[None, :]
    cf[:, cfg.C_LE:cfg.C_LE + 128] = (lp <= l)
    cf[:, cfg.C_GT:cfg.C_GT + 128] = (lp > l)
    cf[:, cfg.C_GE:cfg.C_GE + 128] = (lp >= l)
    cf[:, cfg.C_LT:cfg.C_LT + 128] = (lp < l)
    cf[:, cfg.C_ONE:cfg.C_ONE + 128] = 1.0
    cf[:, cfg.C_IDF:cfg.C_IDF + 128] = np.eye(128)
    cb = np.zeros((128, cfg.NCB * 128), np.float32)
    cb[:, 0:128] = np.eye(128)
    cb[:, cfg.K_GT * 128:(cfg.K_GT + 1) * 128] = (lp > l)
    cb[:, cfg.K_LT * 128:(cfg.K_LT + 1) * 128] = (lp < l)
    cb[:, cfg.K_ONE * 128:(cfg.K_ONE + 1) * 128] = 1.0
    for r4 in range(4):
        cb[:, (cfg.K_NEGF + r4) * 128:(cfg.K_NEGF + r4 + 1) * 128] = np.where(l < lp, -30000.0, 0.0)
        cb[:, (cfg.K_NEGB + r4) * 128:(cfg.K_NEGB + r4 + 1) * 128] = np.where(l > lp, -30000.0, 0.0)
    tok = np.arange(128)
    rr, cc = tok // 64, tok % 64
    for g, w in enumerate(POOL_WINDOWS):
        for d in cfg.dl_s[g]:
            dr = (2 * d + rr[:, None] - rr[None, :])
            dc = (cc[:, None] - cc[None, :])
            m = (dr >= -(w // 2)) & (dr <= w // 2 - 1) & (dc >= -(w // 2)) & (dc <= w // 2 - 1)
            k = cfg.mi_s[(g, d)]
            cb[:, k * 128:(k + 1) * 128] = m
        for d in cfg.dl_p[g]:
            dt = (128 * d + tok[:, None] - tok[None, :])
            m = (dt >= -(w // 2)) & (dt <= w // 2 - 1)
            k = cfg.mi_p[(g, d)]
            cb[:, k * 128:(k + 1) * 128] = m
        lo, hi = _bounds(64, w); cntc = (hi - lo).astype(np.float64)
        lo, hi = _bounds(cfg.ROWS, w); cntr = (hi - lo).astype(np.float64)
        lo, hi = _bounds(cfg.TP, w); cntp = (hi - lo).astype(np.float64)
        cf[:, cfg.C_INVS + g * 128: cfg.C_INVS + (g + 1) * 128] = (1.0 / cntc[cc])[None, :]
        for t in range(2):
            cf[:, cfg.C_INVP + (t * 4 + g) * 128: cfg.C_INVP + (t * 4 + g + 1) * 128] = \
                (1.0 / cntp[128 * t + tok])[None, :]
        for t in range(cfg.NCS):
            cf[:, cfg.C_NEG + t * 4 + g] = -(cntr[2 * t + rr] * cntc[cc])
        for t in range(2):
            cf[:, cfg.C_NEG + (cfg.NCS + t) * 4 + g] = -cntp[128 * t + tok]
    invr = [[1.0 / float(min(r - w // 2 + w, cfg.ROWS) - max(r - w // 2, 0)) for r in range(cfg.ROWS)]
            for w in POOL_WINDOWS]
    return cf, cb, invr


class Buf:
    __slots__ = ("name", "w", "r", "dkey")

    def __init__(s, name, dkey=None):
        s.name = name; s.w = None; s.r = []; s.dkey = dkey if dkey is not None else name


class V:
    __slots__ = ("ap", "buf")

    def __init__(s, ap, buf):
        s.ap = ap; s.buf = buf

    def __getitem__(s, k):
        return V(s.ap[k], s.buf)

    def bitcast(s, dt):
        return V(s.ap.bitcast(dt), s.buf)

    def r(s, pat, **kw):
        return V(s.ap.rearrange(pat, **kw), s.buf)

    def ap_bc(s, n):
        return V(s.ap.broadcast_to([s.ap.shape[0], n]), s.buf)

    def bc(s, axis, n):
        a = s.ap.unsqueeze(axis)
        shp = list(a.shape); shp[axis] = n
        return V(a.broadcast_to(shp), s.buf)


class EngQ:
    def __init__(s, name):
        s.name = name; s.ops = []; s.cnt = 0; s.seen = {}


class Prog:
    def __init__(s):
        s.eng = {n: EngQ(n) for n in ("pe", "act", "dve", "pool", "sp")}
        s.dsems = {}
        s.dkeys = {}
        s.dbufs = {}

    def dbuf(s, key):
        b = s.dbufs.get(key)
        if b is None:
            b = Buf(str(key)); s.dbufs[key] = b
        return b

    def _collect(s, E, reads, writes, skip_self):
        need = {}
        def add(ev):
            if ev is None:
                return
            if skip_self and ev[0] == E.name:
                return
            if need.get(ev[0], 0) < ev[1]:
                need[ev[0]] = ev[1]
        for b in reads:
            add(b.w)
        for b in writes:
            add(b.w)
            for e in b.r:
                add(e)
        out = []
        for k, v in need.items():
            if E.seen.get(k, 0) < v:
                E.seen[k] = v
                out.append((k, v))
        return out

    def op(s, eng, fn, reads=(), writes=()):
        E = s.eng[eng]
        reads = [v.buf for v in reads]; writes = [v.buf for v in writes]
        waits = s._collect(E, reads, writes, skip_self=(eng == "pe"))
        E.cnt += 1
        ev = (E.name, E.cnt)
        E.ops.append((fn, waits, E.name, 1))
        for b in reads:
            b.r.append(ev)
        for b in writes:
            b.w = ev; b.r = []

    def dma(s, q, out, in_, sb, extra=()):
        E = s.eng[q]
        dk = s.dkeys.get(sb.buf.dkey)
        if dk is None:
            dk = "d%d" % len(s.dkeys)
            s.dkeys[sb.buf.dkey] = dk
            s.dsems[dk] = 0
        rds = [in_.buf] + list(extra)
        waits = s._collect(E, rds, [out.buf], skip_self=False)
        prev = s.dsems[dk]
        if prev > 0 and E.seen.get(dk, 0) < prev:
            E.seen[dk] = prev
            waits.append((dk, prev))
        s.dsems[dk] = prev + 16
        ev = (dk, prev + 16)
        oa, ia = out.ap, in_.ap
        E.ops.append((lambda e: e.dma_start(out=oa, in_=ia), waits, dk, 16))
        for b in rds:
            b.r.append(ev)
        out.buf.w = ev; out.buf.r = []

    def inherit(s, newbufs, oldbufs):
        evs = []
        for b in oldbufs:
            if b.w is not None:
                evs.append(b.w)
            evs.extend(b.r)
        for nb in newbufs:
            nb.r = list(evs)


class Arena:
    def __init__(s, P, name, ap_bytes_f32):
        s.P = P; s.name = name; s.base = ap_bytes_f32
        s.nbytes = ap_bytes_f32.shape[1] * 4
        s.cur = []
        s.gen = 0

    def carve(s, specs):
        s.gen += 1
        out = {}; off = 0; newbufs = []
        for (name, dt, shape) in specs:
            esz = 4 if dt == F32 else 2
            n = int(np.prod(shape))
            nb = n * esz
            nb4 = (nb + 3) // 4
            assert off + nb4 * 4 <= s.nbytes, (s.name, name, off, nb, s.nbytes)
            ap = s.base[:, off // 4: off // 4 + nb4]
            if dt != F32:
                ap = ap.bitcast(dt)
            ap = ap[:, 0:n]
            if len(shape) == 2:
                ap = ap.rearrange("p (a b) -> p a b", a=shape[0])
            elif len(shape) == 3:
                ap = ap.rearrange("p (a b c) -> p a b c", a=shape[0], b=shape[1])
            b = Buf("%s.%s.%d" % (s.name, name, s.gen), dkey="%s.%s" % (s.name, name)); newbufs.append(b)
            out[name] = V(ap, b)
            off += ((nb4 * 4 + 31) // 32) * 32
        s.P.inherit(newbufs, s.cur)
        s.cur = newbufs
        return out


def build_program(cfg):
    c = cfg
    nc = bass.Bass("TRN2", target_bir_lowering=False)
    P = Prog()
    D, KD, KF, DI, H, G, HG, GW = c.D, c.KD, c.KF, c.DI, c.H, c.G, c.HG, c.GW
    KC, KX, KG, L, NB, NCH = c.KC, c.KX, c.KG, c.L, c.NB, c.NCH
    cf_np, cb_np, invr = host_consts(c)

    def din(name, shape, dt=F32):
        return nc.dram_tensor(name, list(shape), dt, kind="ExternalInput").ap()

    def dout(name, shape):
        return nc.dram_tensor(name, list(shape), F32, kind="ExternalOutput").ap()

    def dscr(name, shape, dt):
        return nc.dram_tensor(name, list(shape), dt).ap()

    d_x = din("x_in", [NB, 128, KD, 512])
    d_w13 = din("w13", [L, 2, KF // 2, 128, KD, 512])
    d_w2 = din("w2", [L, 2, KD, 128, KF, 128])
    d_win = din("win", [L, len(c.pieces), 128, KD, 512])
    d_wdt = din("wdt", [L, 128, KD, 2 * H])
    d_wbs = din("wbs", [L, KD, 128, KX, 128])
    d_wbp = din("wbp", [L, KD, 128, KD, 128])
    d_wout = din("wout", [L, KD, 128, KD, 128])
    d_pw = din("poolw", [L, 128, 4 * c.GC * c.GC * 128])
    d_ada = din("ada", [L, c.NADA, 128, KD, 512])
    d_vec = din("vecT", [128, c.RV])
    d_rowb = din("rowb", [128, L * c.RB])
    d_cf = din("cf32", [128, c.NCF])
    d_cb = din("cbf", [128, c.NCB * 128])
    d_esel = din("esel", [2 * H, H * 128])
    d_hf0 = din("hf0", [L, 128, DI])
    d_hb0 = din("hb0", [L, 128, DI])
    d_y = dout("y_out", [NB, 128, KD, 512])
    d_stf = dout("st_f", [2, L, 128, DI])
    d_stb = dout("st_b", [2, L, 128, DI])
    s_x1 = dscr("s_x1", [NB, 128, KD, 512], F32)
    s_xn = dscr("s_xn", [NB, 128, KD, 512], F32)
    s_up = dscr("s_up", [NCH, 128, D], BF16)
    s_zs = dscr("s_zs", [NCH, 128, DI], BF16)
    s_xbcr = dscr("s_xbcr", [128, KC, c.TOKP], BF16)
    s_dt = dscr("s_dt", [NCH, 128, 2 * H], F32)
    s_gt = dscr("s_gt", [NB, 128, KG, 512], BF16)
    s_xs = dscr("s_xs", [NCH, 128, DI], BF16)
    s_btk = dscr("s_btk", [NCH, 128, G * 128], BF16)
    s_bct = dscr("s_bct", [NCH, 128, 2 * G, 128], BF16)
    s_hpf = dscr("s_hpf", [NCH, 128, DI], BF16)

    def DR(key, ap):
        return V(ap, P.dbuf(key))

    es = contextlib.ExitStack()
    with es:
        def sb(name, free, dt=F32):
            t = es.enter_context(nc.sbuf_tensor("sb_" + name, [128] + list(free), dt))
            return V(t[:], Buf(name))

        CF = sb("cf", [c.NCF]); CB = sb("cb", [c.NCB * 128], BF16)
        VEC = sb("vec", [c.RV]); ROWB = sb("rowb", [L * c.RB])
        MODT = sb("modt", [2, 9, KD]); MODR = sb("modr", [9 * KD, 2])
        AROW = sb("arow", [2 * H]); CONDT = sb("condt", [KD, 2], BF16)
        MISC = sb("misc", [8])
        HST = sb("hst", [DI]); HSB = sb("hsb", [DI], BF16)
        X = sb("x", [KD, 512]); HH = sb("hh", [KD, 512], BF16)
        big1_bytes = max(KF * 512 * 2, (KX + KD) * 512 * 2, KC * 516 * 2)
        BIG1 = Arena(P, "big1", sb("big1", [big1_bytes // 4 + 8]).ap)
        NWA, NWB = 2, 2
        WA = [sb("wa%d" % i, [KD, 512], BF16) for i in range(NWA)]
        WB = [sb("wb%d" % i, [max(KF, KX), 128], BF16) for i in range(NWB)]
        ch_slot = 2 * DI * 2 + G * 128 * 2 + 2 * G * 128 * 2 + 2 * H * 4 + 160
        ch_bytes = max(2 * ch_slot, KG * 512 * 2)
        CH = Arena(P, "ch", sb("ch", [ch_bytes // 4 + 8]).ap)
        decx_bytes = max(4 * DI * 2 + H * 128 * 2 + 2 * (3 * 4 * H * 4 + 4 * 128 * 2 + G * 128 * 2) + 1024,
                         4 * DI * 2 + 4 * D * 2 + 2 * 4 * 512 * 2 + 256,
                         KC * 512 * 2 + 256,
                         12 * D * 2 + 128, KG * 512 * 2 + 128)
        DECX = Arena(P, "decx", sb("decx", [decx_bytes // 4 + 8]).ap)
        YB = sb("yb", [DI]); YN = sb("yn", [DI], BF16)
        TMP = [sb("tmp%d" % i, [512]) for i in range(6)]
        DEC = [sb("dec%d" % i, [4, 128], BF16) for i in range(2)]
        M4 = [sb("m4%d" % i, [4, 128], BF16) for i in range(3)]
        ESM = sb("esm", [4 * H]); DA = sb("da", [2 * H]); WST = sb("wst", [H])
        MC = [sb("mc%d" % i, [128], BF16) for i in range(2)]
        SSQ = sb("ssq", [8]); DTB = sb("dtb", [2 * H])
        SQB = [TMP[i].bitcast(BF16)[:, 0:512] for i in range(2)]
        DA2 = [sb("da2%d" % i, [2 * H]) for i in range(2)]
        ESM2 = [sb("esm2%d" % i, [4 * H]) for i in range(2)]
        WST2 = [sb("wst2%d" % i, [H]) for i in range(2)]
        DG = [sb("dg%d" % i, [128], BF16) for i in range(8)]
        DTS = [sb("dts%d" % i, [2 * H]) for i in range(4)]
        pst = [es.enter_context(nc.psum_tensor("ps%d" % i, [128, 512], F32)) for i in range(8)]
        PS = [V(t[:], Buf("ps%d" % i)) for i, t in enumerate(pst)]

        IDENT = CB[:, 0:128]
        def cmask(col):
            return CF[:, col:col + 128]
        ONES = cmask(c.C_ONE)
        ONESB = CB[:, c.K_ONE * 128:(c.K_ONE + 1) * 128]
        GTB = CB[:, c.K_GT * 128:(c.K_GT + 1) * 128]
        LTB = CB[:, c.K_LT * 128:(c.K_LT + 1) * 128]
        EPSC = MISC[:, 0:1]; ONEC = MISC[:, 1:2]

        def mm(out, lhsT, rhs, start, stop):
            oa, la, ra = out.ap, lhsT.ap, rhs.ap
            P.op("pe", lambda e: e.matmul(oa, la, ra, start=start, stop=stop, skip_group_check=True),
                 reads=[lhsT, rhs], writes=[out])

        def tr(out, in_):
            oa, ia, ida = out.ap, in_.ap, IDENT.ap
            P.op("pe", lambda e: e.transpose(oa, ia, ida), reads=[in_, IDENT], writes=[out])

        def act(out, in_, func, scale=1.0, bias=0.0, accum=None):
            oa, ia = out.ap, in_.ap
            reads = [in_]; writes = [out]
            kw = {}
            if isinstance(scale, V):
                reads.append(scale); kw["scale"] = scale.ap
            else:
                kw["scale"] = float(scale)
            if isinstance(bias, V):
                reads.append(bias); kw["bias"] = bias.ap
            elif bias != 0.0:
                kw["bias"] = float(bias)
            if accum is not None:
                writes.append(accum); kw["accum_out"] = accum.ap
            P.op("act", lambda e: e.activation(out=oa, in_=ia, func=func, **kw), reads=reads, writes=writes)

        def tt(eng, out, a, b, op):
            oa, aa, ba = out.ap, a.ap, b.ap
            P.op(eng, lambda e: e.tensor_tensor(out=oa, in0=aa, in1=ba, op=op), reads=[a, b], writes=[out])

        def ts(eng, out, a, s1, op0, s2=None, op1=None):
            oa, aa = out.ap, a.ap
            reads = [a]
            s1a = s1.ap if isinstance(s1, V) else float(s1)
            if isinstance(s1, V):
                reads.append(s1)
            if s2 is None:
                P.op(eng, lambda e: e.tensor_scalar(out=oa, in0=aa, scalar1=s1a, scalar2=None, op0=op0),
                     reads=reads, writes=[out])
            else:
                s2a = s2.ap if isinstance(s2, V) else float(s2)
                if isinstance(s2, V):
                    reads.append(s2)
                P.op(eng, lambda e: e.tensor_scalar(out=oa, in0=aa, scalar1=s1a, scalar2=s2a, op0=op0, op1=op1),
                     reads=reads, writes=[out])

        def stt(out, a, sc, b, op0, op1):
            oa, aa, ba = out.ap, a.ap, b.ap
            reads = [a, b]
            sa = sc.ap if isinstance(sc, V) else float(sc)
            if isinstance(sc, V):
                reads.append(sc)
            P.op("dve", lambda e: e.scalar_tensor_tensor(out=oa, in0=aa, scalar=sa, in1=ba, op0=op0, op1=op1),
                 reads=reads, writes=[out])

        def cp(eng, out, in_):
            oa, ia = out.ap, in_.ap
            if eng == "act":
                P.op("act", lambda e: e.copy(out=oa, in_=ia), reads=[in_], writes=[out])
            else:
                P.op(eng, lambda e: e.tensor_copy(out=oa, in_=ia), reads=[in_], writes=[out])

        def memset(eng, out, val):
            oa = out.ap
            P.op(eng, lambda e: e.memset(oa, val), writes=[out])

        def rsum(eng, out, in_):
            oa, ia = out.ap, in_.ap
            P.op(eng, lambda e: e.reduce_sum(out=oa, in_=ia, axis=mybir.AxisListType.X), reads=[in_], writes=[out])

        def recip(out, in_):
            oa, ia = out.ap, in_.ap
            P.op("dve", lambda e: e.reciprocal(out=oa, in_=ia), reads=[in_], writes=[out])

        def ld(out, src, extra=()):
            P.dma("sp", out, src, out, extra)

        def st(dst, in_):
            P.dma("sp", dst, in_, in_)

        def ldc(out, src):
            P.dma("pool", out, src, out)

        conv = {}
        cbufs = [Buf("conv%d" % i) for i in range(6)]

        def convert(key, src_ap):
            dst = dscr("wbf%d" % len(conv), list(src_ap.shape), BF16)
            dv = DR(("wbf",) + tuple(key), dst)
            P.dma("pool", dv, DR(key, src_ap), V(None, cbufs[len(conv) % len(cbufs)]))
            conv[key] = dv

        def ldw(out, key, src_ap):
            if key in conv:
                P.dma("pool", out, conv[key], out)
            else:
                ldc(out, DR(key, src_ap))
                convert(key, src_ap)

        def convert_passA(l):
            for pc in range(KF // 2):
                convert(("w13", l, 0, pc), d_w13[l, 0, pc])
            for m in range(KD):
                convert(("w2", l, 0, m), d_w2[l, 0, m])
            for pi in range(len(c.pieces)):
                convert(("win", l, pi), d_win[l, pi])

        def convert_passC(l):
            for m in range(KD):
                convert(("wbp", l, m), d_wbp[l, m])
                convert(("wbs", l, m), d_wbs[l, m])
            for m in range(KD):
                convert(("wout", l, m), d_wout[l, m])
            for pc in range(KF // 2):
                convert(("w13", l, 1, pc), d_w13[l, 1, pc])
            for m in range(KD):
                convert(("w2", l, 1, m), d_w2[l, 1, m])

        ld(CF, DR("cf", d_cf)); ldc(CB, DR("cb", d_cb))
        ld(VEC, DR("vec", d_vec)); ld(ROWB, DR("rowb", d_rowb))
        memset("dve", MISC[:, 0:1], EPS); memset("dve", MISC[:, 1:2], 1.0); memset("dve", MISC[:, 2:3], -0.5)
        zt = TMP[5].bitcast(BF16)
        memset("dve", zt, 0.0)
        for (c0, n, off, kind) in c.seqs:
            T = n * 128
            for col in (off, off + 2 + T):
                st(DR(("xbcr_halo", col), s_xbcr[:, :, col:col + 2]),
                   zt[:, 0:KC * 2].r("p (a b) -> p a b", a=KC))

        wa_i = [0]; wb_i = [0]
        def next_wa():
            v = WA[wa_i[0] % NWA]; wa_i[0] += 1; return v
        def next_wb():
            v = WB[wb_i[0] % NWB]; wb_i[0] += 1; return v

        def norm_stats(xv, cc, width=512):
            sq = SQB[cc % 2]
            act(sq[:, 0:width], xv[:, cc, :], AF.Square)
            mm(PS[6][:, 0:width], ONESB, sq[:, 0:width], cc == 0, cc == KD - 1)

        def rmsnorm(xv, out_fn, nfeat_chunks, width, acol, scol, have_stats=False):
            if not have_stats:
                for cc in range(KD):
                    norm_stats(xv, cc, width)
            sd = TMP[2]
            act(sd[:, 0:width], PS[6][:, 0:width], AF.Sqrt, scale=1.0 / D, bias=EPSC)
            rstd = TMP[3]
            recip(rstd[:, 0:width], sd[:, 0:width])
            for cc in range(KD):
                t = TMP[cc % 2]
                tt("dve", t[:, 0:width], xv[:, cc, :], rstd[:, 0:width], ALU.mult)
                if scol is None:
                    act(out_fn(cc), t[:, 0:width], AF.Identity, scale=acol(cc))
                else:
                    act(out_fn(cc), t[:, 0:width], AF.Identity, scale=acol(cc), bias=scol(cc))

        def ffn(l, f, gcol, stats=False):
            A = BIG1.carve([("a", BF16, [KF, 512])])["a"]
            for pc in range(KF // 2):
                w = next_wa()
                ldw(w, ("w13", l, f, pc), d_w13[l, f, pc])
                wv = w.r("p k (j c) -> p k j c", j=2)
                for jj in range(2):
                    j = 2 * pc + jj
                    gp, up = PS[(2 * j) % 4], PS[(2 * j + 1) % 4]
                    for k in range(KD):
                        mm(gp, wv[:, k, jj, 0:128], HH[:, k, :], k == 0, k == KD - 1)
                    for k in range(KD):
                        mm(up, wv[:, k, jj, 128:256], HH[:, k, :], k == 0, k == KD - 1)
                    sg = TMP[4 + (j % 2)]
                    act(sg, gp, AF.Silu)
                    tt("dve", A[:, j, :], sg, up, ALU.mult)
            for m in range(KD):
                if m % 2 == 0:
                    w = next_wb()
                else:
                    w = next_wa().r("p k c -> p (k c)")[:, 0:KF * 128].r("p (j c) -> p j c", c=128)
                ldw(w[:, 0:KF, :], ("w2", l, f, m), d_w2[l, f, m])
                op_ = PS[4 + (m % 2)]
                for j in range(KF):
                    mm(op_, w[:, j, :], A[:, j, :], j == 0, j == KF - 1)
                stt(X[:, m, :], op_, gcol(m), X[:, m, :], ALU.mult, ALU.add)
                if stats and m > 0:
                    norm_stats(X, m - 1)
            if stats:
                norm_stats(X, KD - 1)

        def mcol(j, i):
            return lambda cc: MODT[:, j, i, cc:cc + 1]

        def mods(l):
            vb = l * c.VL
            act(CONDT[:, :, 0], VEC[:, c.V_C:c.V_C + KD], AF.Silu)
            act(CONDT[:, :, 1], VEC[:, c.V_CCTX:c.V_CCTX + KD], AF.Silu)
            mp = PS[7][:, 0:9 * KD * 2].r("p (q j) -> p q j", j=2)
            for pc in range(c.NADA):
                w = next_wa()
                ldc(w, DR(("ada", l, pc), d_ada[l, pc]))
                for cc in range(4):
                    q = pc * 4 + cc
                    for k in range(KD):
                        mm(mp[:, q, :], w[:, k, cc * 128:(cc + 1) * 128], CONDT[:, k, :], k == 0, k == KD - 1)
            tt("dve", MODR, mp, VEC[:, vb + c.V_ADAB: vb + c.V_ADAB + 9 * KD].bc(2, 2), ALU.add)
            for j in range(2):
                for i in range(3):
                    shift = MODR[:, (3 * i) * KD:(3 * i + 1) * KD, j]
                    scale = MODR[:, (3 * i + 1) * KD:(3 * i + 2) * KD, j]
                    gate = MODR[:, (3 * i + 2) * KD:(3 * i + 3) * KD, j]
                    ng = VEC[:, vb + c.V_NG + i * KD: vb + c.V_NG + (i + 1) * KD]
                    stt(MODT[:, j, i, :], scale, 1.0, ng, ALU.add, ALU.mult)
                    cp("dve", MODT[:, j, 3 + i, :], shift)
                    ts("dve", MODT[:, j, 6 + i, :], gate, 1.0 if i == 1 else 0.5, ALU.mult)
            act(AROW, ROWB[:, l * c.RB: l * c.RB + 2 * H], AF.Exp)
            ts("dve", AROW, AROW, -1.0, ALU.mult)

        def block_chunks(b):
            return [4 * b + q for q in range(4)]

        def xbcr_dst(b, fc0, nfc, pi_):
            if b < c.NBS:
                off = 2 + 512 * b
                return [(DR(("xbcr", b, pi_), s_xbcr[:, fc0:fc0 + nfc, off:off + 512]), slice(0, 512))]
            out = []
            for i in (1, 2):
                off = c.seqs[i][2] + 2
                out.append((DR(("xbcr", b, i, pi_), s_xbcr[:, fc0:fc0 + nfc, off:off + c.TP]),
                            slice((i - 1) * c.TP, i * c.TP)))
            return out

        xpre = {}

        def passA_load(l, b):
            src = d_x if l == 0 else s_xn
            ld(X, DR(("xin" if l == 0 else "xn", b), src[b]))

        def passA(l, b):
            j = 0 if b < c.NBS else 1
            if not xpre.pop((l, b), False):
                passA_load(l, b)
                for cc in range(KD):
                    norm_stats(X, cc)
            rmsnorm(X, lambda cc: HH[:, cc, :], KD, 512, mcol(j, 0), mcol(j, 3), have_stats=True)
            ffn(l, 0, mcol(j, 6), stats=True)
            st(DR(("x1", b), s_x1[b]), X)
            rmsnorm(X, lambda cc: HH[:, cc, :], KD, 512, mcol(j, 1), mcol(j, 4), have_stats=True)
            if b + 1 < NB:
                passA_load(l, b + 1)
            stg = DECX.carve([("zs", BF16, [4, DI]), ("up", BF16, [4, D]),
                              ("f0", BF16, [4, 512]), ("f1", BF16, [4, 512])])
            chs = block_chunks(b)
            fst = [stg["f0"], stg["f1"]]; fi = 0
            psi = 0
            WDT = next_wa()[:, :, 0:2 * H]
            ldc(WDT, DR(("wdt", l), d_wdt[l]))
            dtb = ROWB[:, l * c.RB + 2 * H: l * c.RB + 4 * H]
            for q in range(4):
                ps = PS[7][:, 0:2 * H]
                for k in range(KD):
                    mm(ps, HH[:, k, q * 128:(q + 1) * 128], WDT[:, k, :], k == 0, k == KD - 1)
                t = DTS[q]
                tt("dve", t, ps, dtb, ALU.add)
                act(t, t, AF.Exp)
                act(t, t, AF.Ln, bias=ONEC)
                ts("dve", t, t, 1e-18, ALU.max)
                st(DR(("dt", chs[q]), s_dt[chs[q]]), t)

            for pi, (seg, si) in enumerate(c.pieces):
                w = next_wa()
                ldw(w, ("win", l, pi), d_win[l, pi])
                if seg in ("up", "z"):
                    for q in range(4):
                        ps = PS[psi % 6]; psi += 1
                        for k in range(KD):
                            mm(ps, HH[:, k, q * 128:(q + 1) * 128], w[:, k, :], k == 0, k == KD - 1)
                        if seg == "up":
                            cp("act", stg["up"][:, q, si * 512:(si + 1) * 512], ps)
                        else:
                            act(stg["zs"][:, q, si * 512:(si + 1) * 512], ps, AF.Silu)
                    if seg == "up" and si == D // 512 - 1:
                        for q in range(4):
                            st(DR(("up", chs[q]), s_up[chs[q]]), stg["up"][:, q, :])
                    if seg == "z" and si == DI // 512 - 1:
                        for q in range(4):
                            st(DR(("zs", chs[q]), s_zs[chs[q]]), stg["zs"][:, q, :])
                else:
                    fs = fst[fi % 2]; fi += 1
                    for cc in range(4):
                        ps = PS[psi % 6]; psi += 1
                        for k in range(KD):
                            mm(ps, w[:, k, cc * 128:(cc + 1) * 128], HH[:, k, :], k == 0, k == KD - 1)
                        if seg == "xbc":
                            cp("dve", fs[:, cc, :], ps)
                        else:
                            act(fs[:, cc, :], ps, AF.Sigmoid)
                    if seg == "xbc":
                        for (dv, tsl) in xbcr_dst(b, si * 4, 4, si):
                            st(dv, fs[:, :, tsl])
                    else:
                        st(DR(("gt", b, si), s_gt[b, :, si * 4:(si + 1) * 4, :]), fs)
            if b + 1 < NB:
                for cc in range(KD):
                    norm_stats(X, cc)
                xpre[(l, b + 1)] = True

        def passB(l, b):
            vb = l * c.VL
            if b < c.NBS:
                pcs = [(0, 4 * b, 4, 512 * b)]
            else:
                pcs = [(1, c.NCS, 2, c.seqs[1][2]), (2, c.NCS + 2, 2, c.seqs[2][2])]
            for (si, ch0, nchk, col0) in pcs:
                n = nchk * 128
                raw = BIG1.carve([("raw", BF16, [KC, 516])])["raw"]
                reg = DECX.carve([("xc", BF16, [KC, 512])])
                reg.update(CH.carve([("xs0", BF16, [DI]), ("xs1", BF16, [DI]),
                                     ("bt0", BF16, [G * 128]), ("bt1", BF16, [G * 128])]))
                xc = reg["xc"]
                npc = c.CONVD // 512
                if b < c.NBS:
                    deps = [P.dbuf(("xbcr", bb, pi_)) for bb in (b - 1, b, b + 1) if 0 <= bb < c.NBS
                            for pi_ in range(npc)]
                else:
                    deps = [P.dbuf(("xbcr", b, si, pi_)) for pi_ in range(npc)]
                deps += [P.dbuf(("xbcr_halo", cc_)) for cc_ in
                         (c.seqs[si][2], c.seqs[si][2] + 2 + c.seqs[si][1] * 128)]
                ld(raw[:, :, 0:n + 3], V(s_xbcr[:, :, col0:col0 + n + 3], Buf("xbcr_rd")), extra=deps)
                for fc in range(KC):
                    cbv = VEC[:, vb + c.V_CB + fc: vb + c.V_CB + fc + 1]
                    dgs = []
                    for k in range(4):
                        wk = VEC[:, vb + c.V_CW + k * KC + fc: vb + c.V_CW + k * KC + fc + 1]
                        dg = DG[(fc % 2) * 4 + k]
                        ts("dve", dg, IDENT, wk, ALU.mult)
                        dgs.append(dg)
                    ps = PS[3 + (fc % 4)][:, 0:n]
                    for k in range(4):
                        mm(ps, dgs[k], raw[:, fc, k:k + n], k == 0, k == 3)
                    act(xc[:, fc, 0:n], ps, AF.Silu, bias=cbv)
                for q in range(nchk):
                    ch = ch0 + q
                    pos = ch - c.seqs[si][0]
                    tsl = slice(q * 128, (q + 1) * 128)
                    st(DR(("bct", ch), s_bct[ch]), xc[:, KX:KX + 2 * G, tsl])
                    xs = reg["xs%d" % (q % 2)]; bt = reg["bt%d" % (q % 2)]
                    for half in range(2):
                        tp = PS[half].bitcast(BF16)
                        for i in range(KX // 2):
                            fc = half * (KX // 2) + i
                            tr(tp[:, i * 128:(i + 1) * 128], xc[:, fc, tsl])
                        cp("act", xs[:, half * (DI // 2):(half + 1) * (DI // 2)], tp[:, 0:DI // 2])
                    tp = PS[2].bitcast(BF16)
                    for g in range(G):
                        tr(tp[:, g * 128:(g + 1) * 128], xc[:, KX + g, tsl])
                    cp("dve", bt, tp[:, 0:G * 128])
                    st(DR(("xs", ch), s_xs[ch]), xs)
                    st(DR(("btk", ch), s_btk[ch]), bt)
                    if pos == 0:
                        if si == 0:
                            ld(HST, DR(("hf0", l), d_hf0[l]))
                        else:
                            memset("dve", HST, 0.0)
                        cp("act", HSB, HST)
                    st(DR(("hpf", ch), s_hpf[ch]), HSB)
                    ld(DTB, DR(("dt", ch), s_dt[ch]))
                    tt("dve", DA[:, 0:H], DTB[:, 0:H], AROW[:, 0:H], ALU.mult)
                    sm = PS[7]
                    mm(sm[:, 0:H], cmask(c.C_GT), DA[:, 0:H], True, True)
                    mm(sm[:, H:2 * H], ONES, DA[:, 0:H], True, True)
                    act(ESM[:, 0:2 * H], sm[:, 0:2 * H], AF.Exp)
                    tt("dve", WST, DTB[:, 0:H], ESM[:, 0:H], ALU.mult)
                    xw = YN
                    tt("dve", xw.r("p (h e) -> p h e", e=64), xs.r("p (h e) -> p h e", e=64),
                       WST.bc(2, 64), ALU.mult)
                    for g in range(G):
                        sp_ = PS[3 + g]
                        mm(sp_[:, 0:GW], bt[:, g * 128:(g + 1) * 128], xw[:, g * GW:(g + 1) * GW], True, True)
                        hv = HST[:, g * GW:(g + 1) * GW]
                        tt("dve", hv.r("p (h e) -> p h e", e=64), hv.r("p (h e) -> p h e", e=64),
                           ESM[:, H + g * HG: H + (g + 1) * HG].bc(2, 64), ALU.mult)
                        tt("dve", hv, hv, sp_[:, 0:GW], ALU.add)
                    cp("dve", HSB, HST)
                    if si > 0 and pos == c.seqs[si][1] - 1:
                        st(DR(("stf", si, l), d_stf[si - 1, l]), HST)

        def make_chunk(l, ch, q, ynT, sl, dx, vb):
            si, pos = c.seq_of(ch)
            nseq = c.seqs[si][1]
            xs, btk, bct, dtt, zs, hpf = sl["xs"], sl["btk"], sl["bct"], sl["dt"], sl["zs"], sl["hpf"]
            da, esm, wst, scs = sl["da"], sl["esm"], sl["wst"], sl["sc"]
            dad, ndad, lnd, ops_, thi = sl["dad"], sl["ndad"], sl["lnd"], sl["ops"], sl["thi"]
            E2 = dx["e2"]
            rb = l * c.RB
            xwb, dxs = dx["xwb"], dx["dxs"]
            y3 = YB
            nq = HG // 4
            its = [(g, d, qq) for g in range(G) for d in range(2) for qq in range(nq)]
            xs3 = xs.r("p (h e) -> p h e", e=64)
            IDF = cmask(c.C_IDF)
            H2 = 2 * H

            def front():
                ld(xs, DR(("xs", ch), s_xs[ch])); ld(btk, DR(("btk", ch), s_btk[ch]))
                ld(bct, DR(("bct", ch), s_bct[ch])); ld(dtt, DR(("dt", ch), s_dt[ch]))
                ld(zs, DR(("zs", ch), s_zs[ch])); ld(hpf, DR(("hpf", ch), s_hpf[ch]))
                tt("dve", da, dtt, AROW, ALU.mult)
                sm = PS[0]
                mm(sm[:, 0:H], cmask(c.C_LE), da[:, 0:H], True, True)
                mm(sm[:, H:2 * H], cmask(c.C_GE), da[:, H:2 * H], True, True)
                mm(sm[:, 2 * H:3 * H], cmask(c.C_LT), da[:, H:2 * H], True, True)
                mm(sm[:, 3 * H:4 * H], ONES, da[:, H:2 * H], True, True)
                act(esm, sm[:, 0:4 * H], AF.Exp)
                tt("dve", wst, dtt[:, H:2 * H], esm[:, 2 * H:3 * H], ALU.mult)
                dad4 = dad.r("p (d r h) -> p d r h", d=2, r=2)
                cp("dve", dad4, da.r("p (d h) -> p d h", d=2).bc(2, 2))
                ts("dve", ndad, dad, -1.0, ALU.mult)
                act(lnd.r("p (d r h) -> p d r h", d=2, r=2), dtt.r("p (d h) -> p d h", d=2).bc(2, 2), AF.Ln)
                fm = PS[7]
                for d in range(2):
                    msk = cmask(c.C_LE) if d == 0 else cmask(c.C_GE)
                    mm(fm[0:H2, d * 256:d * 256 + 128], dad[:, d * H2:(d + 1) * H2], msk, True, True)
                    mm(fm[0:H2, d * 256 + 128:d * 256 + 256], lnd[:, d * H2:(d + 1) * H2], IDF, True, False)
                    mm(fm[0:H2, d * 256 + 128:d * 256 + 256], ndad[:, d * H2:(d + 1) * H2], msk, False, True)
                fm4 = fm[:, 0:512].r("p (k c) -> p k c", k=4)
                cp("dve", ops_[0:H], fm4[0:H])
                cp("dve", thi[H:H2], fm4[H:H2])
                tt("dve", ops_[H:H2], fm4[H:H2], thi[H:H2], ALU.subtract)
                sc = PS[1].r("p (g l) -> p g l", g=G)
                for g in range(G):
                    mm(sc[:, g, :], bct[:, g, :], bct[:, G + g, :], True, True)
                cp("act", scs, sc)

            def front_b():
                tt("dve", dxs.r("p (h e) -> p h e", e=64), xs3,
                   ROWB[:, rb + 4 * H: rb + 5 * H].bc(2, 64), ALU.mult)

            m4s = {}

            def s1(i):
                g, d, qq = its[i]
                h0 = g * HG + qq * 4
                kneg = c.K_NEGF if d == 0 else c.K_NEGB
                sg = PS[2 + (i % 2)]
                mm(sg, IDENT, CB[:, kneg * 128:(kneg + 4) * 128], True, False)
                R = ops_[0:H2, 2 * d, :]; Cc = ops_[0:H2, 2 * d + 1, :]
                for hh in range(4):
                    h = h0 + hh
                    reg = sg[:, hh * 128:(hh + 1) * 128]
                    mm(reg, E2[0:H2, h, :], R, False, False)
                    mm(reg, Cc, E2[0:H2, h, :], False, hh == 3)
                dec = DEC[i % 2]
                act(dec.r("p a b -> p (a b)"), sg, AF.Exp)
                m4 = M4[i % 3]
                tt("dve", m4, dec, scs[:, g, :].bc(1, 4), ALU.mult)
                m4s[i] = m4

            def s2(i):
                g, d, qq = its[i]
                h0 = g * HG + qq * 4
                m4 = m4s[i]
                yps = PS[4 + (g % 2)]
                if d == 0 and qq == 0:
                    mm(yps[:, 0:GW], IDENT, dxs[:, g * GW:(g + 1) * GW], True, False)
                for hh in range(4):
                    h = h0 + hh
                    hl = h - g * HG
                    mm(yps[:, hl * 64:(hl + 1) * 64], m4[:, hh, :], xs[:, h * 64:(h + 1) * 64], False, d == 1)
                if d == 1 and qq == nq - 1:
                    yof = PS[6]
                    mm(yof[:, 0:GW], bct[:, G + g, :], hpf[:, g * GW:(g + 1) * GW], True, True)

            def s3(i):
                g, d, qq = its[i]
                if d == 1 and qq == nq - 1:
                    yps = PS[4 + (g % 2)]; yof = PS[6]
                    t1 = TMP[0][:, 0:GW]
                    tt("dve", t1.r("p (h e) -> p h e", e=64), yof[:, 0:GW].r("p (h e) -> p h e", e=64),
                       esm[:, g * HG:(g + 1) * HG].bc(2, 64), ALU.mult)
                    tt("dve", y3[:, g * GW:(g + 1) * GW], t1, yps[:, 0:GW], ALU.add)

            def main(inject=None, inject_at=7, inject2=None, inject2_at=9):
                n = len(its)
                for step in range(n + 2):
                    if step < n:
                        s1(step)
                    if 1 <= step < n + 1:
                        s2(step - 1)
                    if 2 <= step:
                        s3(step - 2)
                    if inject is not None and step == inject_at:
                        inject()
                    if inject2 is not None and step == inject2_at:
                        inject2()

            def post1():
                if pos == nseq - 1:
                    if si == 0:
                        ld(HST, DR(("hb0", l), d_hb0[l]))
                    else:
                        memset("dve", HST, 0.0)
                    cp("act", HSB, HST)
                t2s = [TMP[1], TMP[2], TMP[4], TMP[5]]
                for g in range(G):
                    yob = PS[7] if g % 2 == 0 else PS[6]
                    gsl = slice(g * GW, (g + 1) * GW)
                    mm(yob[:, 0:GW], bct[:, G + g, :], HSB[:, gsl], True, True)
                    t2 = t2s[g % 4][:, 0:GW]
                    tt("dve", t2.r("p (h e) -> p h e", e=64), yob[:, 0:GW].r("p (h e) -> p h e", e=64),
                       esm[:, H + g * HG: H + (g + 1) * HG].bc(2, 64), ALU.mult)
                for g in range(G):
                    gsl = slice(g * GW, (g + 1) * GW)
                    tt("dve", y3[:, gsl], y3[:, gsl], t2s[g % 4][:, 0:GW], ALU.add)
                tt("dve", y3, y3, zs, ALU.mult)
                act(YN, y3, AF.Square, accum=SSQ[:, 4:5])
                ts("dve", SSQ[:, 5:6], SSQ[:, 4:5], 1.0 / DI, ALU.mult, EPS, ALU.add)
                tt("pool", SSQ[:, 6:7], SSQ[:, 5:6], MISC[:, 2:3], ALU.pow)
                act(YN, y3, AF.Identity, scale=SSQ[:, 6:7])

            def post2():
                sn = VEC[:, vb + c.V_SN: vb + c.V_SN + KX]
                for half in range(2):
                    tp = PS[7].bitcast(BF16)
                    for i in range(KX // 2):
                        fc = half * (KX // 2) + i
                        tr(tp[:, i * 128:(i + 1) * 128], YN[:, fc * 128:(fc + 1) * 128])
                    f0 = half * (KX // 2)
                    tt("dve", ynT[:, f0:f0 + KX // 2, q * 128:(q + 1) * 128],
                       tp[:, 0:(KX // 2) * 128].r("p (a b) -> p a b", b=128),
                       sn[:, f0:f0 + KX // 2].bc(2, 128), ALU.mult)
                tt("pool", xwb.r("p (h e) -> p h e", e=64), xs3, wst.bc(2, 64), ALU.mult)
                for g in range(G):
                    sp_ = PS[g % 2]
                    mm(sp_[:, 0:GW], btk[:, g * 128:(g + 1) * 128], xwb[:, g * GW:(g + 1) * GW], True, True)
                    hv = HST[:, g * GW:(g + 1) * GW]
                    tt("dve", hv.r("p (h e) -> p h e", e=64), hv.r("p (h e) -> p h e", e=64),
                       esm[:, 3 * H + g * HG: 3 * H + (g + 1) * HG].bc(2, 64), ALU.mult)
                    tt("dve", hv, hv, sp_[:, 0:GW], ALU.add)
                cp("dve", HSB, HST)
                if si > 0 and pos == 0:
                    st(DR(("stb", si, l), d_stb[si - 1, l]), HST)

            return front, main, post1, post2, front_b

        def pool_groups(b):
            if b < c.NBS:
                tlo, thi = max(0, 4 * b - 4), min(c.NCS, 4 * b + 8)
                return [(list(range(tlo, thi)), [4 * b + q for q in range(4)])], c.dl_s, c.mi_s, 'S'
            return ([([c.NCS, c.NCS + 1], [c.NCS, c.NCS + 1]), ([c.NCS + 2, c.NCS + 3], [c.NCS + 2, c.NCS + 3])],
                    c.dl_p, c.mi_p, 'P')

        ut_pref = {}

        def load_ut(b):
            ut = DECX.carve([("ut", BF16, [12, D])])["ut"]
            groups = pool_groups(b)[0]
            uslot = {}
            ui = 0
            for (tin_list, tout_list) in groups:
                for tch in tin_list:
                    uslot[tch] = ui
                    ld(ut[:, ui, :], DR(("up", tch), s_up[tch])); ui += 1
            ut_pref[b] = (ut, uslot)

        def pool_phase(l, b, dsb, vb):
            groups, dl, mi, kind = pool_groups(b)
            if b not in ut_pref:
                load_ut(b)
            ut, uslot = ut_pref.pop(b)
            PW = next_wb()[:, 0:4 * c.GC * c.GC, :].r("p (g a b) e -> p g a b e", g=4, a=c.GC)
            ldc(PW, DR(("pw", l), d_pw[l].rearrange("p (g a b e) -> p g a b e", g=4, a=c.GC, b=c.GC)))
            mci = 0
            for (tin_list, tout_list) in groups:
                for tch in tout_list:
                    qcol = (tch - 4 * b) * 128
                    si, pos = c.seq_of(tch)
                    dps2 = [PS[0], PS[1]] if (tch % 2 == 0) else [PS[2], PS[3]]
                    for g in range(4):
                        negc = CF[:, c.C_NEG + ((pos if kind == 'S' else c.NCS + pos) * 4 + g):
                                  c.C_NEG + ((pos if kind == 'S' else c.NCS + pos) * 4 + g) + 1]
                        mcv = MC[mci % 2]; mci += 1
                        k0 = mi[(g, 0)]
                        stt(mcv, IDENT, negc, CB[:, k0 * 128:(k0 + 1) * 128], ALU.mult, ALU.add)
                        for cc in range(c.GC):
                            reg = dps2[(g * c.GC + cc) // 4][:, ((g * c.GC + cc) % 4) * 128:((g * c.GC + cc) % 4 + 1) * 128]
                            dd = [d for d in dl[g] if (tch + d) in uslot and c.seq_of(tch + d)[0] == si]
                            for ii, d in enumerate(dd):
                                k = mi[(g, d)]
                                rhs = mcv if d == 0 else CB[:, k * 128:(k + 1) * 128]
                                mm(reg, ut[:, uslot[tch + d], g * c.GD + cc * 128: g * c.GD + (cc + 1) * 128],
                                   rhs, ii == 0, ii == len(dd) - 1)
                            dst = dsb[:, g * c.GC + cc, qcol:qcol + 128]
                            if kind == 'S':
                                tab = CF[:, c.C_INVS + g * 128: c.C_INVS + (g + 1) * 128]
                                r0, r1 = invr[g][2 * pos], invr[g][2 * pos + 1]
                                if r0 == r1:
                                    stt(dst, reg, r0, tab, ALU.mult, ALU.mult)
                                else:
                                    stt(dst[:, 0:64], reg[:, 0:64], r0, tab[:, 0:64], ALU.mult, ALU.mult)
                                    stt(dst[:, 64:128], reg[:, 64:128], r1, tab[:, 64:128], ALU.mult, ALU.mult)
                            else:
                                tab = CF[:, c.C_INVP + (pos * 4 + g) * 128: c.C_INVP + (pos * 4 + g + 1) * 128]
                                tt("dve", dst, reg, tab, ALU.mult)
            psc = VEC[:, vb + c.V_PS: vb + c.V_PS + KD]
            for g in range(4):
                pss = []
                for dd_ in range(c.GC):
                    ps = PS[4 + dd_]
                    for cc in range(c.GC):
                        mm(ps, PW[:, g, cc, dd_, :], dsb[:, g * c.GC + cc, :], cc == 0, cc == c.GC - 1)
                    pss.append(ps)
                for dd_ in range(c.GC):
                    fo = g * c.GC + dd_
                    act(dsb[:, fo, :], pss[dd_], AF.Identity, scale=psc[:, fo:fo + 1])

        def passC(l, b, last_layer):
            vb = l * c.VL
            j = 0 if b < c.NBS else 1
            ld(X, DR(("x1", b), s_x1[b]))
            bg = BIG1.carve([("ynT", BF16, [KX, 512]), ("dsb", BF16, [KD, 512])])
            ynT, dsb = bg["ynT"], bg["dsb"]
            pool_phase(l, b, dsb, vb)
            spec = []
            for s2_ in range(2):
                spec += [("xs%d" % s2_, BF16, [DI]), ("btk%d" % s2_, BF16, [G * 128]),
                         ("bct%d" % s2_, BF16, [2 * G, 128]), ("dt%d" % s2_, F32, [2 * H]),
                         ("zs%d" % s2_, BF16, [DI])]
            chv = CH.carve(spec)
            dspec = [("xwb", BF16, [DI]), ("dxs", BF16, [DI]), ("hpf0", BF16, [DI]), ("hpf1", BF16, [DI]),
                     ("e2", BF16, [H, 128])]
            for s2_ in range(2):
                dspec += [("dad%d" % s2_, F32, [4 * H]), ("ndad%d" % s2_, F32, [4 * H]), ("lnd%d" % s2_, F32, [4 * H]),
                          ("ops%d" % s2_, BF16, [4, 128]), ("sc%d" % s2_, BF16, [G, 128])]
            dspec += [("thi", BF16, [4, 128])]
            dx = DECX.carve(dspec)
            ldc(dx["e2"][0:2 * H], DR("esel", d_esel.rearrange("p (h m) -> p h m", h=H)))
            chs = block_chunks(b)
            stages = {}
            for q in (3, 2, 1, 0):
                s2_ = q % 2
                sl = {"xs": chv["xs%d" % s2_], "btk": chv["btk%d" % s2_], "bct": chv["bct%d" % s2_],
                      "dt": chv["dt%d" % s2_], "zs": chv["zs%d" % s2_], "hpf": dx["hpf%d" % s2_],
                      "da": DA2[s2_], "esm": ESM2[s2_], "wst": WST2[s2_], "sc": dx["sc%d" % s2_],
                      "dad": dx["dad%d" % s2_], "ndad": dx["ndad%d" % s2_], "lnd": dx["lnd%d" % s2_],
                      "ops": dx["ops%d" % s2_], "thi": dx["thi"]}
                stages[q] = make_chunk(l, chs[q], q, ynT, sl, dx, vb)
            stages[3][0]()
            stages[3][4]()
            stages[3][1](inject2=stages[2][0])
            for q in (2, 1, 0):
                stages[q][4]()
                stages[q + 1][2]()
                stages[q][1](inject=stages[q + 1][3], inject2=(stages[q - 1][0] if q > 0 else None))
            stages[0][2]()
            stages[0][3]()
            gts = DECX.carve([("gt", BF16, [KG, 512])])["gt"]
            ld(gts, V(s_gt[b], Buf("gt_rd")), extra=[P.dbuf(("gt", b, si_)) for si_ in range(2 * D // 512)])
            yp = dsb
            merged = HH
            for m in range(KD):
                w = next_wb()
                ldw(w[:, 0:KD, :], ("wbp", l, m), d_wbp[l, m])
                pp = PS[(2 * m) % 4]
                for k in range(KD):
                    mm(pp, w[:, k, :], yp[:, k, :], k == 0, k == KD - 1)
                w2_ = next_wa().r("p k c -> p (k c)")[:, 0:KX * 128].r("p (k c) -> p k c", c=128)
                ldw(w2_[:, 0:KX, :], ("wbs", l, m), d_wbs[l, m])
                pq = PS[(2 * m + 1) % 4]
                for k in range(KX):
                    mm(pq, w2_[:, k, :], ynT[:, k, :], k == 0, k == KX - 1)
                t1 = TMP[0]; t2 = TMP[1]
                tt("dve", t1, pp, gts[:, m, :], ALU.mult)
                tt("dve", t2, pq, gts[:, KD + m, :], ALU.mult)
                tt("dve", merged[:, m, :], t1, t2, ALU.add)
            for m in range(KD):
                w = next_wb()
                ldw(w[:, 0:KD, :], ("wout", l, m), d_wout[l, m])
                po = PS[4 + (m % 2)]
                for k in range(KD):
                    mm(po, w[:, k, :], merged[:, k, :], k == 0, k == KD - 1)
                stt(X[:, m, :], po, mcol(j, 7)(m), X[:, m, :], ALU.mult, ALU.add)
                if m > 0:
                    norm_stats(X, m - 1)
            norm_stats(X, KD - 1)
            rmsnorm(X, lambda cc: HH[:, cc, :], KD, 512, mcol(j, 2), mcol(j, 5), have_stats=True)
            if b > 0:
                load_ut(b - 1)
            ffn(l, 1, mcol(j, 8), stats=last_layer)
            if not last_layer:
                st(DR(("xn", b), s_xn[b]), X)
            else:
                yo = BIG1.carve([("yo", F32, [KD, 512])])["yo"]
                fn = VEC[:, c.V_FN:c.V_FN + KD]
                rmsnorm(X, lambda cc: yo[:, cc, :], KD, 512, lambda cc: fn[:, cc:cc + 1], None, have_stats=True)
                st(DR(("yout", b), d_y[b]), yo)

        for l in range(L):
            mods(l)
            for b in range(NB):
                passA(l, b)
            for b in range(NB):
                passB(l, b)
            for b in reversed(range(NB)):
                passC(l, b, l == L - 1)

        sems = {}
        for n in list(P.eng.keys()) + list(P.dsems.keys()):
            sems[n] = es.enter_context(nc.semaphore("s_" + n))
        block = es.enter_context(nc.Block())

        def replay(name, e, final=False):
            for (fn, waits, sname, inc) in P.eng[name].ops:
                for (k, v) in waits:
                    e.wait_ge(sems[k], v)
                if fn is not None:
                    fn(e).then_inc(sems[sname], inc)
            if final:
                for k, v in P.dsems.items():
                    if v > 0:
                        e.wait_ge(sems[k], v)
                for k in ("pe", "act", "dve", "pool"):
                    if P.eng[k].cnt > 0:
                        e.wait_ge(sems[k], P.eng[k].cnt)

        @block.tensor
        def _(e):
            replay("pe", e)

        @block.scalar
        def _(e):
            replay("act", e)

        @block.vector
        def _(e):
            replay("dve", e)

        @block.gpsimd
        def _(e):
            replay("pool", e)

        @block.sync
        def _(e):
            replay("sp", e, final=True)
    return nc, P


def _fm(x):
    T, F = x.shape
    return np.ascontiguousarray(x.T.reshape(F // 128, 128, T).transpose(1, 0, 2))


def _wpieces(w, ncols):
    K, N = w.shape
    return np.ascontiguousarray(w.reshape(K // 128, 128, N // ncols, ncols).transpose(2, 1, 0, 3))


def prepare_inputs(cfg, inp, n_cores):
    c = cfg
    L, D, KD, KF, DI, H = c.L, c.D, c.KD, c.KF, c.DI, c.H
    f = lambda a: np.asarray(a, np.float32)
    cf, cb, _ = host_consts(c)
    shared = {}
    w13 = np.zeros((L, 2, KF // 2, 128, KD, 512), np.float32)
    w2 = np.zeros((L, 2, KD, 128, KF, 128), np.float32)
    for l in range(L):
        for fi, (n13, n2) in enumerate((("ffn1_w13", "ffn1_w2"), ("ffn2_w13", "ffn2_w2"))):
            W = f(inp[n13][l])
            g = W[:, :c.DFF].reshape(KD, 128, KF // 2, 2, 128)
            u = W[:, c.DFF:].reshape(KD, 128, KF // 2, 2, 128)
            gu = np.concatenate([g, u], axis=-1)
            w13[l, fi] = gu.transpose(2, 1, 0, 3, 4).reshape(KF // 2, 128, KD, 512)
            w2[l, fi] = _wpieces(f(inp[n2][l]), 128)
    shared["w13"] = w13; shared["w2"] = w2
    s1 = D; s2 = s1 + DI; s3 = s2 + c.CONVD; s4 = s3 + 2 * H
    win = np.zeros((L, len(c.pieces), 128, KD, 512), np.float32)
    wdt = np.zeros((L, 128, KD, 2 * H), np.float32)
    for l in range(L):
        W = f(inp["w_in"][l])
        Wr = np.concatenate([W[:, :s3], W[:, s4:]], axis=1)
        win[l] = _wpieces(Wr, 512)
        wdt[l] = W[:, s3:s4].reshape(KD, 128, 2 * H).transpose(1, 0, 2)
    shared["win"] = win; shared["wdt"] = wdt
    shared["wbs"] = np.stack([_wpieces(f(inp["w_branch_ssd"][l]), 128) for l in range(L)])
    shared["wbp"] = np.stack([_wpieces(f(inp["w_branch_pool"][l]), 128) for l in range(L)])
    shared["wout"] = np.stack([_wpieces(f(inp["w_out"][l]), 128) for l in range(L)])
    pw = f(inp["pool_w"]).reshape(L, 4, c.GC, 128, c.GC, 128).transpose(0, 3, 1, 2, 4, 5)
    shared["poolw"] = np.ascontiguousarray(pw).reshape(L, 128, -1)
    shared["ada"] = np.stack([_wpieces(f(inp["ada_w"][l]), 512) for l in range(L)])
    shared["cf32"] = cf; shared["cbf"] = cb
    es_ = np.zeros((2 * H, H, 128), np.float32)
    for h_ in range(H):
        es_[h_, h_, :] = 1.0; es_[H + h_, h_, :] = 1.0
    shared["esel"] = es_.reshape(2 * H, H * 128)
    rowb = np.zeros((128, L * c.RB), np.float32)
    for l in range(L):
        r = np.concatenate([f(inp["a_log"][l]).reshape(-1), f(inp["dt_bias"][l]).reshape(-1),
                            f(inp["d_skip"][l]).reshape(-1)])
        rowb[:, l * c.RB:(l + 1) * c.RB] = r[None, :]
    shared["rowb"] = rowb

    def colT(v):
        return f(v).reshape(-1, 128).T

    vbase = np.zeros((128, c.RV), np.float32)
    for l in range(L):
        vb = l * c.VL
        vbase[:, vb + c.V_ADAB: vb + c.V_ADAB + 9 * KD] = colT(inp["ada_b"][l])
        vbase[:, vb + c.V_NG: vb + c.V_NG + 3 * KD] = colT(f(inp["norm_g"][l]).reshape(-1))
        vbase[:, vb + c.V_PS: vb + c.V_PS + KD] = colT(inp["pool_scale"][l])
        vbase[:, vb + c.V_CW: vb + c.V_CW + 4 * c.KC] = colT(f(inp["conv_w"][l]).reshape(-1))
        vbase[:, vb + c.V_CB: vb + c.V_CB + c.KC] = colT(inp["conv_b"][l])
        vbase[:, vb + c.V_SN: vb + c.V_SN + c.KX] = colT(inp["ssd_norm"][l])
    vbase[:, c.V_CCTX:c.V_CCTX + KD] = colT(inp["c_ctx"])
    vbase[:, c.V_FN:c.V_FN + KD] = colT(inp["final_norm"])
    maps = []
    xp = f(inp["x_prompt"]); xs_ = f(inp["x_sample"])
    for i in range(n_cores):
        m = dict(shared)
        blocks = [_fm(xs_[i, 512 * b:512 * (b + 1)]) for b in range(c.NBS)]
        blocks.append(_fm(np.concatenate([xp[2 * i], xp[2 * i + 1]], axis=0)))
        m["x_in"] = np.stack(blocks)
        v = vbase.copy()
        v[:, c.V_C:c.V_C + KD] = colT(inp["c"][i])
        m["vecT"] = v
        m["hf0"] = np.ascontiguousarray(f(inp["state_ssd_fwd"][i]).reshape(L, DI, 128).transpose(0, 2, 1))
        m["hb0"] = np.ascontiguousarray(f(inp["state_ssd_bwd"][i]).reshape(L, DI, 128).transpose(0, 2, 1))
        maps.append(m)
    return maps


def assemble(cfg, results, n_cores):
    c = cfg
    yp = np.zeros((2 * n_cores, c.TP, c.D), np.float32)
    ys = np.zeros((n_cores, c.TS, c.D), np.float32)
    sf = np.zeros((2 * n_cores, c.L, c.H, 64, 128), np.float32)
    sbk = np.zeros_like(sf)
    for i, r in enumerate(results):
        y = r["y_out"]
        tm = y.transpose(0, 3, 2, 1).reshape(c.NB, 512, c.D)
        ys[i] = tm[:c.NBS].reshape(c.TS, c.D)
        yp[2 * i] = tm[c.NBS][:c.TP]; yp[2 * i + 1] = tm[c.NBS][c.TP:]
        for s in range(2):
            sf[2 * i + s] = r["st_f"][s].transpose(0, 2, 1).reshape(c.L, c.H, 64, 128)
            sbk[2 * i + s] = r["st_b"][s].transpose(0, 2, 1).reshape(c.L, c.H, 64, 128)
    return yp, ys, sf, sbk


_CACHE = {}


def kernel(**inputs):
    cfg = Cfg()
    n = 8
    if "nc" not in _CACHE:
        _CACHE["nc"] = build_program(cfg)[0]
    maps = prepare_inputs(cfg, inputs, n)
    res = run_bass_kernel_spmd(_CACHE["nc"], maps, core_ids=list(range(n)))
    return assemble(cfg, res.results, n)
```

```python
import contextlib
import numpy as np
import concourse.bass as bass
import concourse.mybir as mybir
from concourse.bass_utils import run_bass_kernel_spmd

F32 = mybir.dt.float32
BF16 = mybir.dt.bfloat16
AF = mybir.ActivationFunctionType
ALU = mybir.AluOpType
EPS = 1e-6
POOL_WINDOWS = (2, 4, 8, 16)


class Cfg:
    def __init__(s, D=1024, DFF=2816, TS=4096, TP=256, L=2):
        s.D = D; s.KD = D // 128; s.DFF = DFF; s.KF = DFF // 128
        s.DI = 2 * D; s.H = s.DI // 64; s.G = 4; s.HG = s.H // 4; s.GW = s.HG * 64
        s.CONVD = s.DI + 2 * 4 * 128; s.KC = s.CONVD // 128; s.KX = s.DI // 128; s.KG = 2 * s.KD
        s.GD = D // 4; s.GC = s.GD // 128
        s.TS = TS; s.TP = TP; s.L = L
        s.NBS = TS // 512; s.NB = s.NBS + 1
        s.NCS = TS // 128; s.NCH = s.NCS + 4
        s.ROWS = TS // 64
        assert 2 * TP == 512 and s.GD % 128 == 0 and s.KF % 2 == 0
        s.TOKP = TS + 4 + 2 * (TP + 4)
        s.seqs = [(0, s.NCS, 0, 'S'), (s.NCS, 2, TS + 4, 'P'), (s.NCS + 2, 2, TS + 4 + TP + 4, 'P')]
        s.pieces = ([('up', i) for i in range(D // 512)] + [('z', i) for i in range(s.DI // 512)] +
                    [('xbc', i) for i in range(s.CONVD // 512)] + [('gt', i) for i in range(2 * D // 512)])
        s.NADA = 9 * D // 512
        s.VL = 9 * s.KD + 3 * s.KD + s.KD + 4 * s.KC + s.KC + s.KX
        s.V_ADAB = 0; s.V_NG = 9 * s.KD; s.V_PS = s.V_NG + 3 * s.KD; s.V_CW = s.V_PS + s.KD
        s.V_CB = s.V_CW + 4 * s.KC; s.V_SN = s.V_CB + s.KC
        s.V_C = L * s.VL; s.V_CCTX = s.V_C + s.KD; s.V_FN = s.V_CCTX + s.KD; s.RV = s.V_FN + s.KD
        s.RB = 5 * s.H
        s.dl_s = [[-1, 0], [-1, 0, 1], [-2, -1, 0, 1, 2], [-4, -3, -2, -1, 0, 1, 2, 3, 4]]
        s.dl_p = [[-1, 0], [-1, 0, 1], [-1, 0, 1], [-1, 0, 1]]
        s.mi_s = {}; s.mi_p = {}
        n = 1
        for g in range(4):
            for d in s.dl_s[g]:
                s.mi_s[(g, d)] = n; n += 1
        for g in range(4):
            for d in s.dl_p[g]:
                s.mi_p[(g, d)] = n; n += 1
        s.K_GT = n; s.K_LT = n + 1; s.K_ONE = n + 2; s.K_NEGF = n + 3; s.K_NEGB = n + 7
        s.NCB = n + 11
        s.C_LE = 0; s.C_GT = 128; s.C_GE = 256; s.C_LT = 384; s.C_ONE = 512
        s.C_INVS = 640; s.C_INVP = s.C_INVS + 4 * 128; s.C_NEG = s.C_INVP + 2 * 4 * 128
        s.C_IDF = s.C_NEG + (s.NCS + 2) * 4
        s.NCF = s.C_IDF + 128

    def seq_of(s, ch):
        for i, (c0, n, off, kind) in enumerate(s.seqs):
            if c0 <= ch < c0 + n:
                return i, ch - c0
        raise ValueError


def _bounds(n, w):
    idx = np.arange(n)
    lo = np.clip(idx - w // 2, 0, n)
    hi = np.clip(idx - w // 2 + w, 0, n)
    return lo, hi


def host_consts(cfg):
    cf = np.zeros((128, cfg.NCF), np.float32)
    i = np.arange(128)
    lp, l = i[:, None], i[None, :]
    cf[:, cfg.C_LE:cfg.C_LE + 128] = (lp <= l)
    cf[:, cfg.C_GT:cfg.C_GT + 128] = (lp > l)
    cf[:, cfg.C_GE:cfg.C_GE + 128] = (lp >= l)
    cf[:, cfg.C_LT:cfg.C_LT + 128] = (lp < l)
    cf[:, cfg.C_ONE:cfg.C_ONE + 128] = 1.0
    cf[:, cfg.C_IDF:cfg.C_IDF + 128] = np.eye(128)
    cb = np.zeros((128, cfg.NCB * 128), np.float32)
    cb[:, 0:128] = np.eye(128)
    cb[:, cfg.K_GT * 128:(cfg.K_GT + 1) * 128] = (lp > l)
    cb[:, cfg.K_LT * 128:(cfg.K_LT + 1) * 128] = (lp < l)
    cb[:, cfg.K_ONE * 128:(cfg.K_ONE + 1) * 128] = 1.0
    for r4 in range(4):
        cb[:, (cfg.K_NEGF + r4) * 128:(cfg.K_NEGF + r4 + 1) * 128] = np.where(l < lp, -30000.0, 0.0)
        cb[:, (cfg.K_NEGB + r4) * 128:(cfg.K_NEGB + r4 + 1) * 128] = np.where(l > lp, -30000.0, 0.0)
    tok = np.arange(128)
    rr, cc = tok // 64, tok % 64
    for g, w in enumerate(POOL_WINDOWS):
        for d in cfg.dl_s[g]:
            dr = (2 * d + rr[:, None] - rr[None, :])
            dc = (cc[:, None] - cc[None, :])
            m = (dr >= -(w // 2)) & (dr <= w // 2 - 1) & (dc >= -(w // 2)) & (dc <= w // 2 - 1)
            k = cfg.mi_s[(g, d)]
            cb[:, k * 128:(k + 1) * 128] = m
        for d in cfg.dl_p[g]:
            dt = (128 * d + tok[:, None] - tok[None, :])
            m = (dt >= -(w // 2)) & (dt <= w // 2 - 1)
            k = cfg.mi_p[(g, d)]
            cb[:, k * 128:(k + 1) * 128] = m
        lo, hi = _bounds(64, w); cntc = (hi - lo).astype(np.float64)
        lo, hi = _bounds(cfg.ROWS, w); cntr = (hi - lo).astype(np.float64)
        lo, hi = _bounds(cfg.TP, w); cntp = (hi - lo).astype(np.float64)
        cf[:, cfg.C_INVS + g * 128: cfg.C_INVS + (g + 1) * 128] = (1.0 / cntc[cc])[None, :]
        for t in range(2):
            cf[:, cfg.C_INVP + (t * 4 + g) * 128: cfg.C_INVP + (t * 4 + g + 1) * 128] = \
                (1.0 / cntp[128 * t + tok])[None, :]
        for t in range(cfg.NCS):
            cf[:, cfg.C_NEG + t * 4 + g] = -(cntr[2 * t + rr] * cntc[cc])
        for t in range(2):
            cf[:, cfg.C_NEG + (cfg.NCS + t) * 4 + g] = -cntp[128 * t + tok]
    invr = [[1.0 / float(min(r - w // 2 + w, cfg.ROWS) - max(r - w // 2, 0)) for r in range(cfg.ROWS)]
            for w in POOL_WINDOWS]
    return cf, cb, invr


class Buf:
    __slots__ = ("name", "w", "r", "dkey")

    def __init__(s, name, dkey=None):
        s.name = name; s.w = None; s.r = []; s.dkey = dkey if dkey is not None else name


class V:
    __slots__ = ("ap", "buf")

    def __init__(s, ap, buf):
        s.ap = ap; s.buf = buf

    def __getitem__(s, k):
        return V(s.ap[k], s.buf)

    def bitcast(s, dt):
        return V(s.ap.bitcast(dt), s.buf)

    def r(s, pat, **kw):
        return V(s.ap.rearrange(pat, **kw), s.buf)

    def ap_bc(s, n):
        return V(s.ap.broadcast_to([s.ap.shape[0], n]), s.buf)

    def bc(s, axis, n):
        a = s.ap.unsqueeze(axis)
        shp = list(a.shape); shp[axis] = n
        return V(a.broadcast_to(shp), s.buf)


class EngQ:
    def __init__(s, name):
        s.name = name; s.ops = []; s.cnt = 0; s.seen = {}


class Prog:
    def __init__(s):
        s.eng = {n: EngQ(n) for n in ("pe", "act", "dve", "pool", "sp")}
        s.dsems = {}
        s.dkeys = {}
        s.dbufs = {}

    def dbuf(s, key):
        b = s.dbufs.get(key)
        if b is None:
            b = Buf(str(key)); s.dbufs[key] = b
        return b

    def _collect(s, E, reads, writes, skip_self):
        need = {}
        def add(ev):
            if ev is None:
                return
            if skip_self and ev[0] == E.name:
                return
            if need.get(ev[0], 0) < ev[1]:
                need[ev[0]] = ev[1]
        for b in reads:
            add(b.w)
        for b in writes:
            add(b.w)
            for e in b.r:
                add(e)
        out = []
        for k, v in need.items():
            if E.seen.get(k, 0) < v:
                E.seen[k] = v
                out.append((k, v))
        return out

    def op(s, eng, fn, reads=(), writes=()):
        E = s.eng[eng]
        reads = [v.buf for v in reads]; writes = [v.buf for v in writes]
        waits = s._collect(E, reads, writes, skip_self=(eng == "pe"))
        E.cnt += 1
        ev = (E.name, E.cnt)
        E.ops.append((fn, waits, E.name, 1))
        for b in reads:
            b.r.append(ev)
        for b in writes:
            b.w = ev; b.r = []

    def dma(s, q, out, in_, sb, extra=()):
        E = s.eng[q]
        dk = s.dkeys.get(sb.buf.dkey)
        if dk is None:
            dk = "d%d" % len(s.dkeys)
            s.dkeys[sb.buf.dkey] = dk
            s.dsems[dk] = 0
        rds = [in_.buf] + list(extra)
        waits = s._collect(E, rds, [out.buf], skip_self=False)
        prev = s.dsems[dk]
        if prev > 0 and E.seen.get(dk, 0) < prev:
            E.seen[dk] = prev
            waits.append((dk, prev))
        s.dsems[dk] = prev + 16
        ev = (dk, prev + 16)
        oa, ia = out.ap, in_.ap
        E.ops.append((lambda e: e.dma_start(out=oa, in_=ia), waits, dk, 16))
        for b in rds:
            b.r.append(ev)
        out.buf.w = ev; out.buf.r = []

    def inherit(s, newbufs, oldbufs):
        evs = []
        for b in oldbufs:
            if b.w is not None:
                evs.append(b.w)
            evs.extend(b.r)
        for nb in newbufs:
            nb.r = list(evs)


class Arena:
    def __init__(s, P, name, ap_bytes_f32):
        s.P = P; s.name = name; s.base = ap_bytes_f32
        s.nbytes = ap_bytes_f32.shape[1] * 4
        s.cur = []
        s.gen = 0

    def carve(s, specs):
        s.gen += 1
        out = {}; off = 0; newbufs = []
        for (name, dt, shape) in specs:
            esz = 4 if dt == F32 else 2
            n = int(np.prod(shape))
            nb = n * esz
            nb4 = (nb + 3) // 4
            assert off + nb4 * 4 <= s.nbytes, (s.name, name, off, nb, s.nbytes)
            ap = s.base[:, off // 4: off // 4 + nb4]
            if dt != F32:
                ap = ap.bitcast(dt)
            ap = ap[:, 0:n]
            if len(shape) == 2:
                ap = ap.rearrange("p (a b) -> p a b", a=shape[0])
            elif len(shape) == 3:
                ap = ap.rearrange("p (a b c) -> p a b c", a=shape[0], b=shape[1])
            b = Buf("%s.%s.%d" % (s.name, name, s.gen), dkey="%s.%s" % (s.name, name)); newbufs.append(b)
            out[name] = V(ap, b)
            off += ((nb4 * 4 + 31) // 32) * 32
        s.P.inherit(newbufs, s.cur)
        s.cur = newbufs
        return out


def build_program(cfg):
    c = cfg
    nc = bass.Bass("TRN2", target_bir_lowering=False)
    P = Prog()
    D, KD, KF, DI, H, G, HG, GW = c.D, c.KD, c.KF, c.DI, c.H, c.G, c.HG, c.GW
    KC, KX, KG, L, NB, NCH = c.KC, c.KX, c.KG, c.L, c.NB, c.NCH
    cf_np, cb_np, invr = host_consts(c)

    def din(name, shape, dt=F32):
        return nc.dram_tensor(name, list(shape), dt, kind="ExternalInput").ap()

    def dout(name, shape):
        return nc.dram_tensor(name, list(shape), F32, kind="ExternalOutput").ap()

    def dscr(name, shape, dt):
        return nc.dram_tensor(name, list(shape), dt).ap()

    d_x = din("x_in", [NB, 128, KD, 512])
    d_w13 = din("w13", [L, 2, KF // 2, 128, KD, 512])
    d_w2 = din("w2", [L, 2, KD, 128, KF, 128])
    d_win = din("win", [L, len(c.pieces), 128, KD, 512])
    d_wdt = din("wdt", [L, 128, KD, 2 * H])
    d_wbs = din("wbs", [L, KD, 128, KX, 128])
    d_wbp = din("wbp", [L, KD, 128, KD, 128])
    d_wout = din("wout", [L, KD, 128, KD, 128])
    d_pw = din("poolw", [L, 128, 4 * c.GC * c.GC * 128])
    d_ada = din("ada", [L, c.NADA, 128, KD, 512])
    d_vec = din("vecT", [128, c.RV])
    d_rowb = din("rowb", [128, L * c.RB])
    d_cf = din("cf32", [128, c.NCF])
    d_cb = din("cbf", [128, c.NCB * 128])
    d_esel = din("esel", [2 * H, H * 128])
    d_hf0 = din("hf0", [L, 128, DI])
    d_hb0 = din("hb0", [L, 128, DI])
    d_y = dout("y_out", [NB, 128, KD, 512])
    d_stf = dout("st_f", [2, L, 128, DI])
    d_stb = dout("st_b", [2, L, 128, DI])
    s_x1 = dscr("s_x1", [NB, 128, KD, 512], F32)
    s_xn = dscr("s_xn", [NB, 128, KD, 512], F32)
    s_up = dscr("s_up", [NCH, 128, D], BF16)
    s_zs = dscr("s_zs", [NCH, 128, DI], BF16)
    s_xbcr = dscr("s_xbcr", [128, KC, c.TOKP], BF16)
    s_dt = dscr("s_dt", [NCH, 128, 2 * H], F32)
    s_gt = dscr("s_gt", [NB, 128, KG, 512], BF16)
    s_xs = dscr("s_xs", [NCH, 128, DI], BF16)
    s_btk = dscr("s_btk", [NCH, 128, G * 128], BF16)
    s_bct = dscr("s_bct", [NCH, 128, 2 * G, 128], BF16)
    s_hpf = dscr("s_hpf", [NCH, 128, DI], BF16)

    def DR(key, ap):
        return V(ap, P.dbuf(key))

    es = contextlib.ExitStack()
    with es:
        def sb(name, free, dt=F32):
            t = es.enter_context(nc.sbuf_tensor("sb_" + name, [128] + list(free), dt))
            return V(t[:], Buf(name))

        CF = sb("cf", [c.NCF]); CB = sb("cb", [c.NCB * 128], BF16)
        VEC = sb("vec", [c.RV]); ROWB = sb("rowb", [L * c.RB])
        MODT = sb("modt", [2, 9, KD]); MODR = sb("modr", [9 * KD, 2])
        AROW = sb("arow", [2 * H]); CONDT = sb("condt", [KD, 2], BF16)
        MISC = sb("misc", [8])
        HST = sb("hst", [DI]); HSB = sb("hsb", [DI], BF16)
        X = sb("x", [KD, 512]); HH = sb("hh", [KD, 512], BF16)
        big1_bytes = max(KF * 512 * 2, (KX + KD) * 512 * 2, KC * 516 * 2)
        BIG1 = Arena(P, "big1", sb("big1", [big1_bytes // 4 + 8]).ap)
        NWA, NWB = 2, 2
        WA = [sb("wa%d" % i, [KD, 512], BF16) for i in range(NWA)]
        WB = [sb("wb%d" % i, [max(KF, KX), 128], BF16) for i in range(NWB)]
        ch_slot = 2 * DI * 2 + G * 128 * 2 + 2 * G * 128 * 2 + 2 * H * 4 + 160
        ch_bytes = max(2 * ch_slot, KG * 512 * 2)
        CH = Arena(P, "ch", sb("ch", [ch_bytes // 4 + 8]).ap)
        decx_bytes = max(4 * DI * 2 + H * 128 * 2 + 2 * (3 * 4 * H * 4 + 4 * 128 * 2 + G * 128 * 2) + 1024,
                         4 * DI * 2 + 4 * D * 2 + 2 * 4 * 512 * 2 + 256,
                         KC * 512 * 2 + 256,
                         12 * D * 2 + 128, KG * 512 * 2 + 128)
        DECX = Arena(P, "decx", sb("decx", [decx_bytes // 4 + 8]).ap)
        YB = sb("yb", [DI]); YN = sb("yn", [DI], BF16)
        TMP = [sb("tmp%d" % i, [512]) for i in range(6)]
        DEC = [sb("dec%d" % i, [4, 128], BF16) for i in range(2)]
        M4 = [sb("m4%d" % i, [4, 128], BF16) for i in range(3)]
        ESM = sb("esm", [4 * H]); DA = sb("da", [2 * H]); WST = sb("wst", [H])
        MC = [sb("mc%d" % i, [128], BF16) for i in range(2)]
        SSQ = sb("ssq", [8]); DTB = sb("dtb", [2 * H])
        SQB = [TMP[i].bitcast(BF16)[:, 0:512] for i in range(2)]
        DA2 = [sb("da2%d" % i, [2 * H]) for i in range(2)]
        ESM2 = [sb("esm2%d" % i, [4 * H]) for i in range(2)]
        WST2 = [sb("wst2%d" % i, [H]) for i in range(2)]
        DG = [sb("dg%d" % i, [128], BF16) for i in range(8)]
        DTS = [sb("dts%d" % i, [2 * H]) for i in range(4)]
        pst = [es.enter_context(nc.psum_tensor("ps%d" % i, [128, 512], F32)) for i in range(8)]
        PS = [V(t[:], Buf("ps%d" % i)) for i, t in enumerate(pst)]

        IDENT = CB[:, 0:128]
        def cmask(col):
            return CF[:, col:col + 128]
        ONES = cmask(c.C_ONE)
        ONESB = CB[:, c.K_ONE * 128:(c.K_ONE + 1) * 128]
        GTB = CB[:, c.K_GT * 128:(c.K_GT + 1) * 128]
        LTB = CB[:, c.K_LT * 128:(c.K_LT + 1) * 128]
        EPSC = MISC[:, 0:1]; ONEC = MISC[:, 1:2]

        def mm(out, lhsT, rhs, start, stop):
            oa, la, ra = out.ap, lhsT.ap, rhs.ap
            P.op("pe", lambda e: e.matmul(oa, la, ra, start=start, stop=stop, skip_group_check=True),
                 reads=[lhsT, rhs], writes=[out])

        def tr(out, in_):
            oa, ia, ida = out.ap, in_.ap, IDENT.ap
            P.op("pe", lambda e: e.transpose(oa, ia, ida), reads=[in_, IDENT], writes=[out])

        def act(out, in_, func, scale=1.0, bias=0.0, accum=None):
            oa, ia = out.ap, in_.ap
            reads = [in_]; writes = [out]
            kw = {}
            if isinstance(scale, V):
                reads.append(scale); kw["scale"] = scale.ap
            else:
                kw["scale"] = float(scale)
            if isinstance(bias, V):
                reads.append(bias); kw["bias"] = bias.ap
            elif bias != 0.0:
                kw["bias"] = float(bias)
            if accum is not None:
                writes.append(accum); kw["accum_out"] = accum.ap
            P.op("act", lambda e: e.activation(out=oa, in_=ia, func=func, **kw), reads=reads, writes=writes)

        def tt(eng, out, a, b, op):
            oa, aa, ba = out.ap, a.ap, b.ap
            P.op(eng, lambda e: e.tensor_tensor(out=oa, in0=aa, in1=ba, op=op), reads=[a, b], writes=[out])

        def ts(eng, out, a, s1, op0, s2=None, op1=None):
            oa, aa = out.ap, a.ap
            reads = [a]
            s1a = s1.ap if isinstance(s1, V) else float(s1)
            if isinstance(s1, V):
                reads.append(s1)
            if s2 is None:
                P.op(eng, lambda e: e.tensor_scalar(out=oa, in0=aa, scalar1=s1a, scalar2=None, op0=op0),
                     reads=reads, writes=[out])
            else:
                s2a = s2.ap if isinstance(s2, V) else float(s2)
                if isinstance(s2, V):
                    reads.append(s2)
                P.op(eng, lambda e: e.tensor_scalar(out=oa, in0=aa, scalar1=s1a, scalar2=s2a, op0=op0, op1=op1),
                     reads=reads, writes=[out])

        def stt(out, a, sc, b, op0, op1):
            oa, aa, ba = out.ap, a.ap, b.ap
            reads = [a, b]
            sa = sc.ap if isinstance(sc, V) else float(sc)
            if isinstance(sc, V):
                reads.append(sc)
            P.op("dve", lambda e: e.scalar_tensor_tensor(out=oa, in0=aa, scalar=sa, in1=ba, op0=op0, op1=op1),
                 reads=reads, writes=[out])

        def cp(eng, out, in_):
            oa, ia = out.ap, in_.ap
            if eng == "act":
                P.op("act", lambda e: e.copy(out=oa, in_=ia), reads=[in_], writes=[out])
            else:
                P.op(eng, lambda e: e.tensor_copy(out=oa, in_=ia), reads=[in_], writes=[out])

        def memset(eng, out, val):
            oa = out.ap
            P.op(eng, lambda e: e.memset(oa, val), writes=[out])

        def rsum(eng, out, in_):
            oa, ia = out.ap, in_.ap
            P.op(eng, lambda e: e.reduce_sum(out=oa, in_=ia, axis=mybir.AxisListType.X), reads=[in_], writes=[out])

        def recip(out, in_):
            oa, ia = out.ap, in_.ap
            P.op("dve", lambda e: e.reciprocal(out=oa, in_=ia), reads=[in_], writes=[out])

        def ld(out, src, extra=()):
            P.dma("sp", out, src, out, extra)

        def st(dst, in_):
            P.dma("sp", dst, in_, in_)

        def stp(dst, in_):
            P.dma("pool", dst, in_, in_)

        def ldc(out, src):
            P.dma("pool", out, src, out)

        conv = {}
        cbufs = [Buf("conv%d" % i) for i in range(6)]

        def convert(key, src_ap):
            dst = dscr("wbf%d" % len(conv), list(src_ap.shape), BF16)
            dv = DR(("wbf",) + tuple(key), dst)
            P.dma("pool", dv, DR(key, src_ap), V(None, cbufs[len(conv) % len(cbufs)]))
            conv[key] = dv

        def ldw(out, key, src_ap):
            if key in conv:
                P.dma("pool", out, conv[key], out)
            else:
                ldc(out, DR(key, src_ap))
                convert(key, src_ap)

        def convert_passA(l):
            for pc in range(KF // 2):
                convert(("w13", l, 0, pc), d_w13[l, 0, pc])
            for m in range(KD):
                convert(("w2", l, 0, m), d_w2[l, 0, m])
            for pi in range(len(c.pieces)):
                convert(("win", l, pi), d_win[l, pi])

        def convert_passC(l):
            for m in range(KD):
                convert(("wbp", l, m), d_wbp[l, m])
                convert(("wbs", l, m), d_wbs[l, m])
            for m in range(KD):
                convert(("wout", l, m), d_wout[l, m])
            for pc in range(KF // 2):
                convert(("w13", l, 1, pc), d_w13[l, 1, pc])
            for m in range(KD):
                convert(("w2", l, 1, m), d_w2[l, 1, m])

        ld(CF, DR("cf", d_cf)); ldc(CB, DR("cb", d_cb))
        ld(VEC, DR("vec", d_vec)); ld(ROWB, DR("rowb", d_rowb))
        memset("dve", MISC[:, 0:1], EPS); memset("dve", MISC[:, 1:2], 1.0); memset("dve", MISC[:, 2:3], -0.5)
        zt = TMP[5].bitcast(BF16)
        memset("dve", zt, 0.0)
        for (c0, n, off, kind) in c.seqs:
            T = n * 128
            for col in (off, off + 2 + T):
                st(DR(("xbcr_halo", col), s_xbcr[:, :, col:col + 2]),
                   zt[:, 0:KC * 2].r("p (a b) -> p a b", a=KC))

        wa_i = [0]; wb_i = [0]
        def next_wa():
            v = WA[wa_i[0] % NWA]; wa_i[0] += 1; return v
        def next_wb():
            v = WB[wb_i[0] % NWB]; wb_i[0] += 1; return v

        def norm_stats(xv, cc, width=512):
            sq = SQB[cc % 2]
            act(sq[:, 0:width], xv[:, cc, :], AF.Square)
            mm(PS[6][:, 0:width], ONESB, sq[:, 0:width], cc == 0, cc == KD - 1)

        def rmsnorm(xv, out_fn, nfeat_chunks, width, acol, scol, have_stats=False):
            if not have_stats:
                for cc in range(KD):
                    norm_stats(xv, cc, width)
            sd = TMP[2]
            act(sd[:, 0:width], PS[6][:, 0:width], AF.Sqrt, scale=1.0 / D, bias=EPSC)
            rstd = TMP[3]
            recip(rstd[:, 0:width], sd[:, 0:width])
            for cc in range(KD):
                t = TMP[cc % 2]
                tt("dve", t[:, 0:width], xv[:, cc, :], rstd[:, 0:width], ALU.mult)
                if scol is None:
                    act(out_fn(cc), t[:, 0:width], AF.Identity, scale=acol(cc))
                else:
                    act(out_fn(cc), t[:, 0:width], AF.Identity, scale=acol(cc), bias=scol(cc))

        def ffn(l, f, gcol, stats=False):
            A = BIG1.carve([("a", BF16, [KF, 512])])["a"]
            for pc in range(KF // 2):
                w = next_wa()
                ldw(w, ("w13", l, f, pc), d_w13[l, f, pc])
                wv = w.r("p k (j c) -> p k j c", j=2)
                for jj in range(2):
                    j = 2 * pc + jj
                    gp, up = PS[(2 * j) % 4], PS[(2 * j + 1) % 4]
                    for k in range(KD):
                        mm(gp, wv[:, k, jj, 0:128], HH[:, k, :], k == 0, k == KD - 1)
                    for k in range(KD):
                        mm(up, wv[:, k, jj, 128:256], HH[:, k, :], k == 0, k == KD - 1)
                    sg = TMP[4 + (j % 2)]
                    act(sg, gp, AF.Silu)
                    tt("dve", A[:, j, :], sg, up, ALU.mult)
            for m in range(KD):
                if m % 2 == 0:
                    w = next_wb()
                else:
                    w = next_wa().r("p k c -> p (k c)")[:, 0:KF * 128].r("p (j c) -> p j c", c=128)
                ldw(w[:, 0:KF, :], ("w2", l, f, m), d_w2[l, f, m])
                op_ = PS[4 + (m % 2)]
                for j in range(KF):
                    mm(op_, w[:, j, :], A[:, j, :], j == 0, j == KF - 1)
                stt(X[:, m, :], op_, gcol(m), X[:, m, :], ALU.mult, ALU.add)
                if stats and m > 0:
                    norm_stats(X, m - 1)
            if stats:
                norm_stats(X, KD - 1)

        def mcol(j, i):
            return lambda cc: MODT[:, j, i, cc:cc + 1]

        def mods(l):
            vb = l * c.VL
            act(CONDT[:, :, 0], VEC[:, c.V_C:c.V_C + KD], AF.Silu)
            act(CONDT[:, :, 1], VEC[:, c.V_CCTX:c.V_CCTX + KD], AF.Silu)
            mp = PS[7][:, 0:9 * KD * 2].r("p (q j) -> p q j", j=2)
            for pc in range(c.NADA):
                w = next_wa()
                ldc(w, DR(("ada", l, pc), d_ada[l, pc]))
                for cc in range(4):
                    q = pc * 4 + cc
                    for k in range(KD):
                        mm(mp[:, q, :], w[:, k, cc * 128:(cc + 1) * 128], CONDT[:, k, :], k == 0, k == KD - 1)
            tt("dve", MODR, mp, VEC[:, vb + c.V_ADAB: vb + c.V_ADAB + 9 * KD].bc(2, 2), ALU.add)
            for j in range(2):
                for i in range(3):
                    shift = MODR[:, (3 * i) * KD:(3 * i + 1) * KD, j]
                    scale = MODR[:, (3 * i + 1) * KD:(3 * i + 2) * KD, j]
                    gate = MODR[:, (3 * i + 2) * KD:(3 * i + 3) * KD, j]
                    ng = VEC[:, vb + c.V_NG + i * KD: vb + c.V_NG + (i + 1) * KD]
                    stt(MODT[:, j, i, :], scale, 1.0, ng, ALU.add, ALU.mult)
                    cp("dve", MODT[:, j, 3 + i, :], shift)
                    ts("dve", MODT[:, j, 6 + i, :], gate, 1.0 if i == 1 else 0.5, ALU.mult)
            act(AROW, ROWB[:, l * c.RB: l * c.RB + 2 * H], AF.Exp)
            ts("dve", AROW, AROW, -1.0, ALU.mult)

        def block_chunks(b):
            return [4 * b + q for q in range(4)]

        def xbcr_dst(b, fc0, nfc, pi_):
            if b < c.NBS:
                off = 2 + 512 * b
                return [(DR(("xbcr", b, pi_), s_xbcr[:, fc0:fc0 + nfc, off:off + 512]), slice(0, 512))]
            out = []
            for i in (1, 2):
                off = c.seqs[i][2] + 2
                out.append((DR(("xbcr", b, i, pi_), s_xbcr[:, fc0:fc0 + nfc, off:off + c.TP]),
                            slice((i - 1) * c.TP, i * c.TP)))
            return out

        xpre = {}

        def passA_load(l, b):
            src = d_x if l == 0 else s_xn
            ld(X, DR(("xin" if l == 0 else "xn", b), src[b]))

        def passA(l, b):
            j = 0 if b < c.NBS else 1
            if not xpre.pop((l, b), False):
                passA_load(l, b)
                for cc in range(KD):
                    norm_stats(X, cc)
            rmsnorm(X, lambda cc: HH[:, cc, :], KD, 512, mcol(j, 0), mcol(j, 3), have_stats=True)
            ffn(l, 0, mcol(j, 6), stats=True)
            st(DR(("x1", b), s_x1[b]), X)
            rmsnorm(X, lambda cc: HH[:, cc, :], KD, 512, mcol(j, 1), mcol(j, 4), have_stats=True)
            if b + 1 < NB:
                passA_load(l, b + 1)
            stg = DECX.carve([("zs", BF16, [4, DI]), ("up", BF16, [4, D]),
                              ("f0", BF16, [4, 512]), ("f1", BF16, [4, 512])])
            chs = block_chunks(b)
            fst = [stg["f0"], stg["f1"]]; fi = 0
            psi = 0
            WDT = next_wa()[:, :, 0:2 * H]
            ldc(WDT, DR(("wdt", l), d_wdt[l]))
            dtb = ROWB[:, l * c.RB + 2 * H: l * c.RB + 4 * H]
            for q in range(4):
                ps = PS[7][:, 0:2 * H]
                for k in range(KD):
                    mm(ps, HH[:, k, q * 128:(q + 1) * 128], WDT[:, k, :], k == 0, k == KD - 1)
                t = DTS[q]
                tt("dve", t, ps, dtb, ALU.add)
                act(t, t, AF.Exp)
                act(t, t, AF.Ln, bias=ONEC)
                ts("dve", t, t, 1e-18, ALU.max)
                st(DR(("dt", chs[q]), s_dt[chs[q]]), t)

            for pi, (seg, si) in enumerate(c.pieces):
                w = next_wa()
                ldw(w, ("win", l, pi), d_win[l, pi])
                if seg in ("up", "z"):
                    for q in range(4):
                        ps = PS[psi % 6]; psi += 1
                        for k in range(KD):
                            mm(ps, HH[:, k, q * 128:(q + 1) * 128], w[:, k, :], k == 0, k == KD - 1)
                        if seg == "up":
                            cp("act", stg["up"][:, q, si * 512:(si + 1) * 512], ps)
                        else:
                            act(stg["zs"][:, q, si * 512:(si + 1) * 512], ps, AF.Silu)
                    if seg == "up" and si == D // 512 - 1:
                        for q in range(4):
                            st(DR(("up", chs[q]), s_up[chs[q]]), stg["up"][:, q, :])
                    if seg == "z" and si == DI // 512 - 1:
                        for q in range(4):
                            st(DR(("zs", chs[q]), s_zs[chs[q]]), stg["zs"][:, q, :])
                else:
                    fs = fst[fi % 2]; fi += 1
                    for cc in range(4):
                        ps = PS[psi % 6]; psi += 1
                        for k in range(KD):
                            mm(ps, w[:, k, cc * 128:(cc + 1) * 128], HH[:, k, :], k == 0, k == KD - 1)
                        if seg == "xbc":
                            cp("dve", fs[:, cc, :], ps)
                        else:
                            act(fs[:, cc, :], ps, AF.Sigmoid)
                    if seg == "xbc":
                        for (dv, tsl) in xbcr_dst(b, si * 4, 4, si):
                            st(dv, fs[:, :, tsl])
                    else:
                        st(DR(("gt", b, si), s_gt[b, :, si * 4:(si + 1) * 4, :]), fs)
            if b + 1 < NB:
                for cc in range(KD):
                    norm_stats(X, cc)
                xpre[(l, b + 1)] = True

        def passB(l, b):
            vb = l * c.VL
            if b < c.NBS:
                pcs = [(0, 4 * b, 4, 512 * b)]
            else:
                pcs = [(1, c.NCS, 2, c.seqs[1][2]), (2, c.NCS + 2, 2, c.seqs[2][2])]
            for (si, ch0, nchk, col0) in pcs:
                n = nchk * 128
                raw = BIG1.carve([("raw", BF16, [KC, 516])])["raw"]
                reg = DECX.carve([("xc", BF16, [KC, 512])])
                reg.update(CH.carve([("xs0", BF16, [DI]), ("xs1", BF16, [DI]),
                                     ("bt0", BF16, [G * 128]), ("bt1", BF16, [G * 128])]))
                xc = reg["xc"]
                npc = c.CONVD // 512
                if b < c.NBS:
                    deps = [P.dbuf(("xbcr", bb, pi_)) for bb in (b - 1, b, b + 1) if 0 <= bb < c.NBS
                            for pi_ in range(npc)]
                else:
                    deps = [P.dbuf(("xbcr", b, si, pi_)) for pi_ in range(npc)]
                deps += [P.dbuf(("xbcr_halo", cc_)) for cc_ in
                         (c.seqs[si][2], c.seqs[si][2] + 2 + c.seqs[si][1] * 128)]
                ld(raw[:, :, 0:n + 3], V(s_xbcr[:, :, col0:col0 + n + 3], Buf("xbcr_rd")), extra=deps)
                for fc in range(KC):
                    cbv = VEC[:, vb + c.V_CB + fc: vb + c.V_CB + fc + 1]
                    dgs = []
                    for k in range(4):
                        wk = VEC[:, vb + c.V_CW + k * KC + fc: vb + c.V_CW + k * KC + fc + 1]
                        dg = DG[(fc % 2) * 4 + k]
                        ts("dve", dg, IDENT, wk, ALU.mult)
                        dgs.append(dg)
                    ps = PS[3 + (fc % 4)][:, 0:n]
                    for k in range(4):
                        mm(ps, dgs[k], raw[:, fc, k:k + n], k == 0, k == 3)
                    act(xc[:, fc, 0:n], ps, AF.Silu, bias=cbv)
                for q in range(nchk):
                    ch = ch0 + q
                    pos = ch - c.seqs[si][0]
                    tsl = slice(q * 128, (q + 1) * 128)
                    ld(DTB, DR(("dt", ch), s_dt[ch]))
                    stp(DR(("bct", ch), s_bct[ch]), xc[:, KX:KX + 2 * G, tsl])
                    xs = reg["xs%d" % (q % 2)]; bt = reg["bt%d" % (q % 2)]
                    for half in range(2):
                        tp = PS[half].bitcast(BF16)
                        for i in range(KX // 2):
                            fc = half * (KX // 2) + i
                            tr(tp[:, i * 128:(i + 1) * 128], xc[:, fc, tsl])
                        cp("act", xs[:, half * (DI // 2):(half + 1) * (DI // 2)], tp[:, 0:DI // 2])
                    tp = PS[2].bitcast(BF16)
                    for g in range(G):
                        tr(tp[:, g * 128:(g + 1) * 128], xc[:, KX + g, tsl])
                    cp("dve", bt, tp[:, 0:G * 128])
                    stp(DR(("xs", ch), s_xs[ch]), xs)
                    stp(DR(("btk", ch), s_btk[ch]), bt)
                    if pos == 0:
                        if si == 0:
                            ld(HST, DR(("hf0", l), d_hf0[l]))
                        else:
                            memset("dve", HST, 0.0)
                        cp("act", HSB, HST)
                    stp(DR(("hpf", ch), s_hpf[ch]), HSB)
                    tt("dve", DA[:, 0:H], DTB[:, 0:H], AROW[:, 0:H], ALU.mult)
                    sm = PS[7]
                    mm(sm[:, 0:H], cmask(c.C_GT), DA[:, 0:H], True, True)
                    mm(sm[:, H:2 * H], ONES, DA[:, 0:H], True, True)
                    act(ESM[:, 0:2 * H], sm[:, 0:2 * H], AF.Exp)
                    tt("dve", WST, DTB[:, 0:H], ESM[:, 0:H], ALU.mult)
                    xw = YN
                    tt("dve", xw.r("p (h e) -> p h e", e=64), xs.r("p (h e) -> p h e", e=64),
                       WST.bc(2, 64), ALU.mult)
                    for g in range(G):
                        sp_ = PS[3 + g]
                        mm(sp_[:, 0:GW], bt[:, g * 128:(g + 1) * 128], xw[:, g * GW:(g + 1) * GW], True, True)
                        hv = HST[:, g * GW:(g + 1) * GW]
                        tt("dve", hv.r("p (h e) -> p h e", e=64), hv.r("p (h e) -> p h e", e=64),
                           ESM[:, H + g * HG: H + (g + 1) * HG].bc(2, 64), ALU.mult)
                        tt("dve", hv, hv, sp_[:, 0:GW], ALU.add)
                    cp("dve", HSB, HST)
                    if si > 0 and pos == c.seqs[si][1] - 1:
                        st(DR(("stf", si, l), d_stf[si - 1, l]), HST)

        def make_chunk(l, ch, q, ynT, sl, dx, vb):
            si, pos = c.seq_of(ch)
            nseq = c.seqs[si][1]
            xs, btk, bct, dtt, zs, hpf = sl["xs"], sl["btk"], sl["bct"], sl["dt"], sl["zs"], sl["hpf"]
            da, esm, wst, scs = sl["da"], sl["esm"], sl["wst"], sl["sc"]
            dad, ndad, lnd, ops_, thi = sl["dad"], sl["ndad"], sl["lnd"], sl["ops"], sl["thi"]
            E2 = dx["e2"]
            rb = l * c.RB
            xwb, dxs = dx["xwb"], dx["dxs"]
            y3 = YB
            nq = HG // 4
            its = [(g, d, qq) for g in range(G) for d in range(2) for qq in range(nq)]
            xs3 = xs.r("p (h e) -> p h e", e=64)
            IDF = cmask(c.C_IDF)
            H2 = 2 * H

            def front():
                ld(xs, DR(("xs", ch), s_xs[ch])); ld(btk, DR(("btk", ch), s_btk[ch]))
                ld(bct, DR(("bct", ch), s_bct[ch])); ld(dtt, DR(("dt", ch), s_dt[ch]))
                P.dma("pool", zs, DR(("zs", ch), s_zs[ch]), zs); P.dma("pool", hpf, DR(("hpf", ch), s_hpf[ch]), hpf)
                tt("dve", da, dtt, AROW, ALU.mult)
                sm = PS[0]
                mm(sm[:, 0:H], cmask(c.C_LE), da[:, 0:H], True, True)
                mm(sm[:, H:2 * H], cmask(c.C_GE), da[:, H:2 * H], True, True)
                mm(sm[:, 2 * H:3 * H], cmask(c.C_LT), da[:, H:2 * H], True, True)
                mm(sm[:, 3 * H:4 * H], ONES, da[:, H:2 * H], True, True)
                act(esm, sm[:, 0:4 * H], AF.Exp)
                tt("dve", wst, dtt[:, H:2 * H], esm[:, 2 * H:3 * H], ALU.mult)
                dad4 = dad.r("p (d r h) -> p d r h", d=2, r=2)
                cp("dve", dad4, da.r("p (d h) -> p d h", d=2).bc(2, 2))
                ts("dve", ndad, dad, -1.0, ALU.mult)
                act(lnd.r("p (d r h) -> p d r h", d=2, r=2), dtt.r("p (d h) -> p d h", d=2).bc(2, 2), AF.Ln)
                fm = PS[7]
                for d in range(2):
                    msk = cmask(c.C_LE) if d == 0 else cmask(c.C_GE)
                    mm(fm[0:H2, d * 256:d * 256 + 128], dad[:, d * H2:(d + 1) * H2], msk, True, True)
                    mm(fm[0:H2, d * 256 + 128:d * 256 + 256], lnd[:, d * H2:(d + 1) * H2], IDF, True, False)
                    mm(fm[0:H2, d * 256 + 128:d * 256 + 256], ndad[:, d * H2:(d + 1) * H2], msk, False, True)
                fm4 = fm[:, 0:512].r("p (k c) -> p k c", k=4)
                cp("dve", ops_[0:H], fm4[0:H])
                cp("dve", thi[H:H2], fm4[H:H2])
                tt("dve", ops_[H:H2], fm4[H:H2], thi[H:H2], ALU.subtract)
                sc = PS[1].r("p (g l) -> p g l", g=G)
                for g in range(G):
                    mm(sc[:, g, :], bct[:, g, :], bct[:, G + g, :], True, True)
                cp("act", scs, sc)

            def front_b():
                tt("dve", dxs.r("p (h e) -> p h e", e=64), xs3,
                   ROWB[:, rb + 4 * H: rb + 5 * H].bc(2, 64), ALU.mult)

            m4s = {}

            def s1(i):
                g, d, qq = its[i]
                h0 = g * HG + qq * 4
                kneg = c.K_NEGF if d == 0 else c.K_NEGB
                sg = PS[2 + (i % 2)]
                mm(sg, IDENT, CB[:, kneg * 128:(kneg + 4) * 128], True, False)
                R = ops_[0:H2, 2 * d, :]; Cc = ops_[0:H2, 2 * d + 1, :]
                for hh in range(4):
                    h = h0 + hh
                    reg = sg[:, hh * 128:(hh + 1) * 128]
                    mm(reg, E2[0:H2, h, :], R, False, False)
                    mm(reg, Cc, E2[0:H2, h, :], False, hh == 3)
                dec = DEC[i % 2]
                act(dec.r("p a b -> p (a b)"), sg, AF.Exp)
                m4 = M4[i % 3]
                tt("dve", m4, dec, scs[:, g, :].bc(1, 4), ALU.mult)
                m4s[i] = m4

            def s2(i):
                g, d, qq = its[i]
                h0 = g * HG + qq * 4
                m4 = m4s[i]
                yps = PS[4 + (g % 2)]
                if d == 0 and qq == 0:
                    mm(yps[:, 0:GW], IDENT, dxs[:, g * GW:(g + 1) * GW], True, False)
                for hh in range(4):
                    h = h0 + hh
                    hl = h - g * HG
                    mm(yps[:, hl * 64:(hl + 1) * 64], m4[:, hh, :], xs[:, h * 64:(h + 1) * 64], False, d == 1)
                if d == 1 and qq == nq - 1:
                    yof = PS[6]
                    mm(yof[:, 0:GW], bct[:, G + g, :], hpf[:, g * GW:(g + 1) * GW], True, True)

            def s3(i):
                g, d, qq = its[i]
                if d == 1 and qq == nq - 1:
                    yps = PS[4 + (g % 2)]; yof = PS[6]
                    t1 = TMP[0][:, 0:GW]
                    tt("dve", t1.r("p (h e) -> p h e", e=64), yof[:, 0:GW].r("p (h e) -> p h e", e=64),
                       esm[:, g * HG:(g + 1) * HG].bc(2, 64), ALU.mult)
                    tt("dve", y3[:, g * GW:(g + 1) * GW], t1, yps[:, 0:GW], ALU.add)

            def main(inject=None, inject_at=7, inject2=None, inject2_at=9):
                n = len(its)
                for step in range(n + 2):
                    if step < n:
                        s1(step)
                    if 1 <= step < n + 1:
                        s2(step - 1)
                    if 2 <= step:
                        s3(step - 2)
                    if inject is not None and step == inject_at:
                        inject()
                    if inject2 is not None and step == inject2_at:
                        inject2()

            def post1():
                if pos == nseq - 1:
                    if si == 0:
                        ld(HST, DR(("hb0", l), d_hb0[l]))
                    else:
                        memset("dve", HST, 0.0)
                    cp("act", HSB, HST)
                t2s = [TMP[1], TMP[2], TMP[4], TMP[5]]
                for g in range(G):
                    yob = PS[7] if g % 2 == 0 else PS[6]
                    gsl = slice(g * GW, (g + 1) * GW)
                    mm(yob[:, 0:GW], bct[:, G + g, :], HSB[:, gsl], True, True)
                    t2 = t2s[g % 4][:, 0:GW]
                    tt("dve", t2.r("p (h e) -> p h e", e=64), yob[:, 0:GW].r("p (h e) -> p h e", e=64),
                       esm[:, H + g * HG: H + (g + 1) * HG].bc(2, 64), ALU.mult)
                for g in range(G):
                    gsl = slice(g * GW, (g + 1) * GW)
                    tt("dve", y3[:, gsl], y3[:, gsl], t2s[g % 4][:, 0:GW], ALU.add)
                tt("dve", y3, y3, zs, ALU.mult)
                act(YN, y3, AF.Square, accum=SSQ[:, 4:5])
                ts("dve", SSQ[:, 5:6], SSQ[:, 4:5], 1.0 / DI, ALU.mult, EPS, ALU.add)
                tt("pool", SSQ[:, 6:7], SSQ[:, 5:6], MISC[:, 2:3], ALU.pow)
                act(YN, y3, AF.Identity, scale=SSQ[:, 6:7])

            def post2():
                sn = VEC[:, vb + c.V_SN: vb + c.V_SN + KX]
                for half in range(2):
                    tp = PS[7].bitcast(BF16)
                    for i in range(KX // 2):
                        fc = half * (KX // 2) + i
                        tr(tp[:, i * 128:(i + 1) * 128], YN[:, fc * 128:(fc + 1) * 128])
                    f0 = half * (KX // 2)
                    tt("dve", ynT[:, f0:f0 + KX // 2, q * 128:(q + 1) * 128],
                       tp[:, 0:(KX // 2) * 128].r("p (a b) -> p a b", b=128),
                       sn[:, f0:f0 + KX // 2].bc(2, 128), ALU.mult)
                tt("pool", xwb.r("p (h e) -> p h e", e=64), xs3, wst.bc(2, 64), ALU.mult)
                for g in range(G):
                    sp_ = PS[g % 2]
                    mm(sp_[:, 0:GW], btk[:, g * 128:(g + 1) * 128], xwb[:, g * GW:(g + 1) * GW], True, True)
                    hv = HST[:, g * GW:(g + 1) * GW]
                    tt("dve", hv.r("p (h e) -> p h e", e=64), hv.r("p (h e) -> p h e", e=64),
                       esm[:, 3 * H + g * HG: 3 * H + (g + 1) * HG].bc(2, 64), ALU.mult)
                    tt("dve", hv, hv, sp_[:, 0:GW], ALU.add)
                cp("dve", HSB, HST)
                if si > 0 and pos == 0:
                    st(DR(("stb", si, l), d_stb[si - 1, l]), HST)

            return front, main, post1, post2, front_b

        def pool_groups(b):
            if b < c.NBS:
                tlo, thi = max(0, 4 * b - 4), min(c.NCS, 4 * b + 8)
                return [(list(range(tlo, thi)), [4 * b + q for q in range(4)])], c.dl_s, c.mi_s, 'S'
            return ([([c.NCS, c.NCS + 1], [c.NCS, c.NCS + 1]), ([c.NCS + 2, c.NCS + 3], [c.NCS + 2, c.NCS + 3])],
                    c.dl_p, c.mi_p, 'P')

        ut_pref = {}

        def load_ut(b):
            ut = DECX.carve([("ut", BF16, [12, D])])["ut"]
            groups = pool_groups(b)[0]
            uslot = {}
            ui = 0
            for (tin_list, tout_list) in groups:
                for tch in tin_list:
                    uslot[tch] = ui
                    ld(ut[:, ui, :], DR(("up", tch), s_up[tch])); ui += 1
            ut_pref[b] = (ut, uslot)

        def pool_phase(l, b, dsb, vb):
            groups, dl, mi, kind = pool_groups(b)
            if b not in ut_pref:
                load_ut(b)
            ut, uslot = ut_pref.pop(b)
            PW = next_wb()[:, 0:4 * c.GC * c.GC, :].r("p (g a b) e -> p g a b e", g=4, a=c.GC)
            ldc(PW, DR(("pw", l), d_pw[l].rearrange("p (g a b e) -> p g a b e", g=4, a=c.GC, b=c.GC)))
            mci = 0
            for (tin_list, tout_list) in groups:
                for tch in tout_list:
                    qcol = (tch - 4 * b) * 128
                    si, pos = c.seq_of(tch)
                    dps2 = [PS[0], PS[1]] if (tch % 2 == 0) else [PS[2], PS[3]]
                    for g in range(4):
                        negc = CF[:, c.C_NEG + ((pos if kind == 'S' else c.NCS + pos) * 4 + g):
                                  c.C_NEG + ((pos if kind == 'S' else c.NCS + pos) * 4 + g) + 1]
                        mcv = MC[mci % 2]; mci += 1
                        k0 = mi[(g, 0)]
                        stt(mcv, IDENT, negc, CB[:, k0 * 128:(k0 + 1) * 128], ALU.mult, ALU.add)
                        for cc in range(c.GC):
                            reg = dps2[(g * c.GC + cc) // 4][:, ((g * c.GC + cc) % 4) * 128:((g * c.GC + cc) % 4 + 1) * 128]
                            dd = [d for d in dl[g] if (tch + d) in uslot and c.seq_of(tch + d)[0] == si]
                            for ii, d in enumerate(dd):
                                k = mi[(g, d)]
                                rhs = mcv if d == 0 else CB[:, k * 128:(k + 1) * 128]
                                mm(reg, ut[:, uslot[tch + d], g * c.GD + cc * 128: g * c.GD + (cc + 1) * 128],
                                   rhs, ii == 0, ii == len(dd) - 1)
                            dst = dsb[:, g * c.GC + cc, qcol:qcol + 128]
                            if kind == 'S':
                                tab = CF[:, c.C_INVS + g * 128: c.C_INVS + (g + 1) * 128]
                                r0, r1 = invr[g][2 * pos], invr[g][2 * pos + 1]
                                if r0 == r1:
                                    stt(dst, reg, r0, tab, ALU.mult, ALU.mult)
                                else:
                                    stt(dst[:, 0:64], reg[:, 0:64], r0, tab[:, 0:64], ALU.mult, ALU.mult)
                                    stt(dst[:, 64:128], reg[:, 64:128], r1, tab[:, 64:128], ALU.mult, ALU.mult)
                            else:
                                tab = CF[:, c.C_INVP + (pos * 4 + g) * 128: c.C_INVP + (pos * 4 + g + 1) * 128]
                                tt("dve", dst, reg, tab, ALU.mult)
            psc = VEC[:, vb + c.V_PS: vb + c.V_PS + KD]
            for g in range(4):
                pss = []
                for dd_ in range(c.GC):
                    ps = PS[4 + dd_]
                    for cc in range(c.GC):
                        mm(ps, PW[:, g, cc, dd_, :], dsb[:, g * c.GC + cc, :], cc == 0, cc == c.GC - 1)
                    pss.append(ps)
                for dd_ in range(c.GC):
                    fo = g * c.GC + dd_
                    act(dsb[:, fo, :], pss[dd_], AF.Identity, scale=psc[:, fo:fo + 1])

        def passC(l, b, last_layer):
            vb = l * c.VL
            j = 0 if b < c.NBS else 1
            ld(X, DR(("x1", b), s_x1[b]))
            bg = BIG1.carve([("ynT", BF16, [KX, 512]), ("dsb", BF16, [KD, 512])])
            ynT, dsb = bg["ynT"], bg["dsb"]
            pool_phase(l, b, dsb, vb)
            spec = []
            for s2_ in range(2):
                spec += [("xs%d" % s2_, BF16, [DI]), ("btk%d" % s2_, BF16, [G * 128]),
                         ("bct%d" % s2_, BF16, [2 * G, 128]), ("dt%d" % s2_, F32, [2 * H]),
                         ("zs%d" % s2_, BF16, [DI])]
            chv = CH.carve(spec)
            dspec = [("xwb", BF16, [DI]), ("dxs", BF16, [DI]), ("hpf0", BF16, [DI]), ("hpf1", BF16, [DI]),
                     ("e2", BF16, [H, 128])]
            for s2_ in range(2):
                dspec += [("dad%d" % s2_, F32, [4 * H]), ("ndad%d" % s2_, F32, [4 * H]), ("lnd%d" % s2_, F32, [4 * H]),
                          ("ops%d" % s2_, BF16, [4, 128]), ("sc%d" % s2_, BF16, [G, 128])]
            dspec += [("thi", BF16, [4, 128])]
            dx = DECX.carve(dspec)
            ldc(dx["e2"][0:2 * H], DR("esel", d_esel.rearrange("p (h m) -> p h m", h=H)))
            chs = block_chunks(b)
            stages = {}
            for q in (3, 2, 1, 0):
                s2_ = q % 2
                sl = {"xs": chv["xs%d" % s2_], "btk": chv["btk%d" % s2_], "bct": chv["bct%d" % s2_],
                      "dt": chv["dt%d" % s2_], "zs": chv["zs%d" % s2_], "hpf": dx["hpf%d" % s2_],
                      "da": DA2[s2_], "esm": ESM2[s2_], "wst": WST2[s2_], "sc": dx["sc%d" % s2_],
                      "dad": dx["dad%d" % s2_], "ndad": dx["ndad%d" % s2_], "lnd": dx["lnd%d" % s2_],
                      "ops": dx["ops%d" % s2_], "thi": dx["thi"]}
                stages[q] = make_chunk(l, chs[q], q, ynT, sl, dx, vb)
            stages[3][0]()
            stages[3][4]()
            stages[3][1](inject2=stages[2][0])
            for q in (2, 1, 0):
                stages[q][4]()
                stages[q + 1][2]()
                stages[q][1](inject=stages[q + 1][3], inject2=(stages[q - 1][0] if q > 0 else None))
            stages[0][2]()
            stages[0][3]()
            gts = DECX.carve([("gt", BF16, [KG, 512])])["gt"]
            ld(gts, V(s_gt[b], Buf("gt_rd")), extra=[P.dbuf(("gt", b, si_)) for si_ in range(2 * D // 512)])
            yp = dsb
            merged = HH
            for m in range(KD):
                w = next_wb()
                ldw(w[:, 0:KD, :], ("wbp", l, m), d_wbp[l, m])
                pp = PS[(2 * m) % 4]
                for k in range(KD):
                    mm(pp, w[:, k, :], yp[:, k, :], k == 0, k == KD - 1)
                w2_ = next_wa().r("p k c -> p (k c)")[:, 0:KX * 128].r("p (k c) -> p k c", c=128)
                ldw(w2_[:, 0:KX, :], ("wbs", l, m), d_wbs[l, m])
                pq = PS[(2 * m + 1) % 4]
                for k in range(KX):
                    mm(pq, w2_[:, k, :], ynT[:, k, :], k == 0, k == KX - 1)
                t1 = TMP[0]; t2 = TMP[1]
                tt("dve", t1, pp, gts[:, m, :], ALU.mult)
                tt("dve", t2, pq, gts[:, KD + m, :], ALU.mult)
                tt("dve", merged[:, m, :], t1, t2, ALU.add)
            for m in range(KD):
                w = next_wb()
                ldw(w[:, 0:KD, :], ("wout", l, m), d_wout[l, m])
                po = PS[4 + (m % 2)]
                for k in range(KD):
                    mm(po, w[:, k, :], merged[:, k, :], k == 0, k == KD - 1)
                stt(X[:, m, :], po, mcol(j, 7)(m), X[:, m, :], ALU.mult, ALU.add)
                if m > 0:
                    norm_stats(X, m - 1)
            norm_stats(X, KD - 1)
            rmsnorm(X, lambda cc: HH[:, cc, :], KD, 512, mcol(j, 2), mcol(j, 5), have_stats=True)
            if b > 0:
                load_ut(b - 1)
            ffn(l, 1, mcol(j, 8), stats=last_layer)
            if not last_layer:
                st(DR(("xn", b), s_xn[b]), X)
            else:
                yo = BIG1.carve([("yo", F32, [KD, 512])])["yo"]
                fn = VEC[:, c.V_FN:c.V_FN + KD]
                rmsnorm(X, lambda cc: yo[:, cc, :], KD, 512, lambda cc: fn[:, cc:cc + 1], None, have_stats=True)
                st(DR(("yout", b), d_y[b]), yo)

        for l in range(L):
            mods(l)
            for b in range(NB):
                passA(l, b)
            for b in range(NB):
                passB(l, b)
            for b in reversed(range(NB)):
                passC(l, b, l == L - 1)

        sems = {}
        for n in list(P.eng.keys()) + list(P.dsems.keys()):
            sems[n] = es.enter_context(nc.semaphore("s_" + n))
        block = es.enter_context(nc.Block())

        def replay(name, e, final=False):
            for (fn, waits, sname, inc) in P.eng[name].ops:
                for (k, v) in waits:
                    e.wait_ge(sems[k], v)
                if fn is not None:
                    fn(e).then_inc(sems[sname], inc)
            if final:
                for k, v in P.dsems.items():
                    if v > 0:
                        e.wait_ge(sems[k], v)
                for k in ("pe", "act", "dve", "pool"):
                    if P.eng[k].cnt > 0:
                        e.wait_ge(sems[k], P.eng[k].cnt)

        @block.tensor
        def _(e):
            replay("pe", e)

        @block.scalar
        def _(e):
            replay("act", e)

        @block.vector
        def _(e):
            replay("dve", e)

        @block.gpsimd
        def _(e):
            replay("pool", e)

        @block.sync
        def _(e):
            replay("sp", e, final=True)
    return nc, P


def _fm(x):
    T, F = x.shape
    return np.ascontiguousarray(x.T.reshape(F // 128, 128, T).transpose(1, 0, 2))


def _wpieces(w, ncols):
    K, N = w.shape
    return np.ascontiguousarray(w.reshape(K // 128, 128, N // ncols, ncols).transpose(2, 1, 0, 3))


def prepare_inputs(cfg, inp, n_cores):
    c = cfg
    L, D, KD, KF, DI, H = c.L, c.D, c.KD, c.KF, c.DI, c.H
    f = lambda a: np.asarray(a, np.float32)
    cf, cb, _ = host_consts(c)
    shared = {}
    w13 = np.zeros((L, 2, KF // 2, 128, KD, 512), np.float32)
    w2 = np.zeros((L, 2, KD, 128, KF, 128), np.float32)
    for l in range(L):
        for fi, (n13, n2) in enumerate((("ffn1_w13", "ffn1_w2"), ("ffn2_w13", "ffn2_w2"))):
            W = f(inp[n13][l])
            g = W[:, :c.DFF].reshape(KD, 128, KF // 2, 2, 128)
            u = W[:, c.DFF:].reshape(KD, 128, KF // 2, 2, 128)
            gu = np.concatenate([g, u], axis=-1)
            w13[l, fi] = gu.transpose(2, 1, 0, 3, 4).reshape(KF // 2, 128, KD, 512)
            w2[l, fi] = _wpieces(f(inp[n2][l]), 128)
    shared["w13"] = w13; shared["w2"] = w2
    s1 = D; s2 = s1 + DI; s3 = s2 + c.CONVD; s4 = s3 + 2 * H
    win = np.zeros((L, len(c.pieces), 128, KD, 512), np.float32)
    wdt = np.zeros((L, 128, KD, 2 * H), np.float32)
    for l in range(L):
        W = f(inp["w_in"][l])
        Wr = np.concatenate([W[:, :s3], W[:, s4:]], axis=1)
        win[l] = _wpieces(Wr, 512)
        wdt[l] = W[:, s3:s4].reshape(KD, 128, 2 * H).transpose(1, 0, 2)
    shared["win"] = win; shared["wdt"] = wdt
    shared["wbs"] = np.stack([_wpieces(f(inp["w_branch_ssd"][l]), 128) for l in range(L)])
    shared["wbp"] = np.stack([_wpieces(f(inp["w_branch_pool"][l]), 128) for l in range(L)])
    shared["wout"] = np.stack([_wpieces(f(inp["w_out"][l]), 128) for l in range(L)])
    pw = f(inp["pool_w"]).reshape(L, 4, c.GC, 128, c.GC, 128).transpose(0, 3, 1, 2, 4, 5)
    shared["poolw"] = np.ascontiguousarray(pw).reshape(L, 128, -1)
    shared["ada"] = np.stack([_wpieces(f(inp["ada_w"][l]), 512) for l in range(L)])
    shared["cf32"] = cf; shared["cbf"] = cb
    es_ = np.zeros((2 * H, H, 128), np.float32)
    for h_ in range(H):
        es_[h_, h_, :] = 1.0; es_[H + h_, h_, :] = 1.0
    shared["esel"] = es_.reshape(2 * H, H * 128)
    rowb = np.zeros((128, L * c.RB), np.float32)
    for l in range(L):
        r = np.concatenate([f(inp["a_log"][l]).reshape(-1), f(inp["dt_bias"][l]).reshape(-1),
                            f(inp["d_skip"][l]).reshape(-1)])
        rowb[:, l * c.RB:(l + 1) * c.RB] = r[None, :]
    shared["rowb"] = rowb

    def colT(v):
        return f(v).reshape(-1, 128).T

    vbase = np.zeros((128, c.RV), np.float32)
    for l in range(L):
        vb = l * c.VL
        vbase[:, vb + c.V_ADAB: vb + c.V_ADAB + 9 * KD] = colT(inp["ada_b"][l])
        vbase[:, vb + c.V_NG: vb + c.V_NG + 3 * KD] = colT(f(inp["norm_g"][l]).reshape(-1))
        vbase[:, vb + c.V_PS: vb + c.V_PS + KD] = colT(inp["pool_scale"][l])
        vbase[:, vb + c.V_CW: vb + c.V_CW + 4 * c.KC] = colT(f(inp["conv_w"][l]).reshape(-1))
        vbase[:, vb + c.V_CB: vb + c.V_CB + c.KC] = colT(inp["conv_b"][l])
        vbase[:, vb + c.V_SN: vb + c.V_SN + c.KX] = colT(inp["ssd_norm"][l])
    vbase[:, c.V_CCTX:c.V_CCTX + KD] = colT(inp["c_ctx"])
    vbase[:, c.V_FN:c.V_FN + KD] = colT(inp["final_norm"])
    maps = []
    xp = f(inp["x_prompt"]); xs_ = f(inp["x_sample"])
    for i in range(n_cores):
        m = dict(shared)
        blocks = [_fm(xs_[i, 512 * b:512 * (b + 1)]) for b in range(c.NBS)]
        blocks.append(_fm(np.concatenate([xp[2 * i], xp[2 * i + 1]], axis=0)))
        m["x_in"] = np.stack(blocks)
        v = vbase.copy()
        v[:, c.V_C:c.V_C + KD] = colT(inp["c"][i])
        m["vecT"] = v
        m["hf0"] = np.ascontiguousarray(f(inp["state_ssd_fwd"][i]).reshape(L, DI, 128).transpose(0, 2, 1))
        m["hb0"] = np.ascontiguousarray(f(inp["state_ssd_bwd"][i]).reshape(L, DI, 128).transpose(0, 2, 1))
        maps.append(m)
    return maps


def assemble(cfg, results, n_cores):
    c = cfg
    yp = np.zeros((2 * n_cores, c.TP, c.D), np.float32)
    ys = np.zeros((n_cores, c.TS, c.D), np.float32)
    sf = np.zeros((2 * n_cores, c.L, c.H, 64, 128), np.float32)
    sbk = np.zeros_like(sf)
    for i, r in enumerate(results):
        y = r["y_out"]
        tm = y.transpose(0, 3, 2, 1).reshape(c.NB, 512, c.D)
        ys[i] = tm[:c.NBS].reshape(c.TS, c.D)
        yp[2 * i] = tm[c.NBS][:c.TP]; yp[2 * i + 1] = tm[c.NBS][c.TP:]
        for s in range(2):
            sf[2 * i + s] = r["st_f"][s].transpose(0, 2, 1).reshape(c.L, c.H, 64, 128)
            sbk[2 * i + s] = r["st_b"][s].transpose(0, 2, 1).reshape(c.L, c.H, 64, 128)
    return yp, ys, sf, sbk


_CACHE = {}


def kernel(**inputs):
    cfg = Cfg()
    n = 8
    if "nc" not in _CACHE:
        _CACHE["nc"] = build_program(cfg)[0]
    maps = prepare_inputs(cfg, inputs, n)
    res = run_bass_kernel_spmd(_CACHE["nc"], maps, core_ids=list(range(n)))
    return assemble(cfg, res.results, n)
```
